# Optimizing a Trainium2 kernel written in Bass

```python
import jax, jax.numpy as jnp
from jax import lax
import numpy as np

D_MODEL = 1024
BATCH = 4
SEQ = 8192
DEPTH = 1
DEC_BATCH = 8
DEC_SEQ = 16
PAST_LEN = 4096

CHUNK = 64
EPS = 1e-6
LRU_WIDTH = D_MODEL // 2
LRU_BLOCKS = 8
LRU_BLOCK_W = LRU_WIDTH // LRU_BLOCKS
CONV_W = 4
LRU_C = 8.0
N_HEADS = 8
N_KV_HEADS = 2
HEAD_DIM = 64
ATTN_WIDTH = N_HEADS * HEAD_DIM
N_IDX_HEADS = 8
IDX_DIM = 64
TOPK_MAX = 256
Q_BLOCK = 128
ROPE_THETA = 10000.0
D_FF = ((8 * D_MODEL // 3 + 255) // 256) * 256
PROJ_SIZES = (LRU_WIDTH, LRU_WIDTH, N_HEADS * HEAD_DIM, N_KV_HEADS * HEAD_DIM,
              N_KV_HEADS * HEAD_DIM, N_IDX_HEADS * IDX_DIM, IDX_DIM, N_IDX_HEADS)
D_IN = sum(PROJ_SIZES)

kernel_name = "hymba_rglru_dsa_streaming_step"


def rmsnorm(x, g):
    xf = x.astype(jnp.float32)
    y = xf * lax.rsqrt(jnp.mean(xf * xf, axis=-1, keepdims=True) + EPS)
    return (y * g.astype(jnp.float32)).astype(x.dtype)


def rope(x, pos):
    half = x.shape[-1] // 2
    inv = ROPE_THETA ** (-jnp.arange(half, dtype=jnp.float32) / half)
    ang = pos.astype(jnp.float32)[:, None] * inv[None, :]
    cos = jnp.cos(ang)[None, :, None, :]
    sin = jnp.sin(ang)[None, :, None, :]
    xf = x.astype(jnp.float32)
    x1, x2 = xf[..., :half], xf[..., half:]
    return jnp.concatenate([x1 * cos - x2 * sin, x2 * cos + x1 * sin], axis=-1).astype(x.dtype)


def causal_conv(xr, prev, w, b):
    T = xr.shape[1]
    xpad = jnp.concatenate([prev.astype(xr.dtype), xr], axis=1)
    y = b
    for j in range(CONV_W):
        y = y + xpad[:, j:j + T] * w[j]
    return y, xpad[:, -(CONV_W - 1):]


def _lin_combine(c1, c2):
    a1, b1 = c1
    a2, b2 = c2
    return a1 * a2, a2 * b1 + b2


def rglru(x, pos, h_prev, w_rg, b_rg, w_ig, b_ig, lru_lambda):
    B, T, W = x.shape
    xf = x.astype(jnp.float32)
    xb = xf.reshape(B, T, LRU_BLOCKS, LRU_BLOCK_W)
    r = jax.nn.sigmoid(jnp.einsum('btni,nij->btnj', xb, w_rg.astype(jnp.float32)) + b_rg.astype(jnp.float32)).reshape(B, T, W)
    i = jax.nn.sigmoid(jnp.einsum('btni,nij->btnj', xb, w_ig.astype(jnp.float32)) + b_ig.astype(jnp.float32)).reshape(B, T, W)
    log_a = -LRU_C * r * jax.nn.softplus(-lru_lambda.astype(jnp.float32))
    a = jnp.exp(log_a)
    mult = jnp.where((pos == 0)[None, :, None], 1.0, jnp.sqrt(-jnp.expm1(2.0 * log_a)))
    bterm = mult * i * xf
    a_cum, b_cum = lax.associative_scan(_lin_combine, (a, bterm), axis=1)
    h = a_cum * h_prev.astype(jnp.float32)[:, None, :] + b_cum
    return h, h[:, -1]


def dsa_block(q, qi, wi, q_pos, k, v, ki, k_pos, topk):
    B, Tq = q.shape[:2]
    sc = jnp.einsum('bthd,bsd->bths', qi.astype(jnp.float32), ki.astype(jnp.float32))
    idx_score = jnp.einsum('bths,bth->bts', jax.nn.relu(sc), wi.astype(jnp.float32))
    adm = (q_pos[:, None] // CHUNK) >= (k_pos[None, :] // CHUNK)
    idx_score = jnp.where(adm[None], idx_score, -jnp.inf)
    vals, sel = lax.top_k(idx_score, topk)
    valid = jnp.isfinite(vals)
    gather = jax.vmap(lambda arr, ix: arr[ix])
    ks = gather(k, sel)
    vs = gather(v, sel)
    qg = q.reshape(B, Tq, N_KV_HEADS, N_HEADS // N_KV_HEADS, HEAD_DIM)
    s = jnp.einsum('bthgd,btkhd->bthgk', qg.astype(jnp.float32), ks.astype(jnp.float32)) * (HEAD_DIM ** -0.5)
    s = jnp.where(valid[:, :, None, None, :], s, -jnp.inf)
    p = jax.nn.softmax(s, axis=-1)
    o = jnp.einsum('bthgk,btkhd->bthgd', p, vs.astype(jnp.float32))
    return o.reshape(B, Tq, N_HEADS * HEAD_DIM).astype(q.dtype)


def dsa_attend(q, qi, wi, q_pos, k, v, ki, k_pos, topk):
    B, T = q.shape[:2]
    if T > Q_BLOCK and T % Q_BLOCK == 0:
        nb = T // Q_BLOCK

        def split_blocks(arr):
            return jnp.moveaxis(arr.reshape((B, nb, Q_BLOCK) + arr.shape[2:]), 1, 0)

        def one(args):
            qb, qib, wib, pb = args
            return dsa_block(qb, qib, wib, pb, k, v, ki, k_pos, topk)

        out = lax.map(one, (split_blocks(q), split_blocks(qi), split_blocks(wi), q_pos.reshape(nb, Q_BLOCK)))
        return jnp.moveaxis(out, 0, 1).reshape(B, T, N_HEADS * HEAD_DIM)
    return dsa_block(q, qi, wi, q_pos, k, v, ki, k_pos, topk)


def layer(x, pos, conv_prev, h_prev, k_past, v_past, ki_past,
          norm_mix, w_in, conv_w, conv_b, w_rg, b_rg, w_ig, b_ig, lru_lambda,
          q_norm, k_norm, w_out, norm_ffn, w_ffn_in, w_ffn_out):
    B, T, _ = x.shape
    h = rmsnorm(x, norm_mix)
    proj = h @ w_in
    parts = []
    off = 0
    for size in PROJ_SIZES:
        parts.append(proj[..., off:off + size])
        off += size
    xr, gate, q, k, v, qi, ki, wi = parts
    xc, conv_new = causal_conv(xr, conv_prev, conv_w, conv_b)
    hs, h_last = rglru(xc, pos, h_prev, w_rg, b_rg, w_ig, b_ig, lru_lambda)
    y_a = hs.astype(x.dtype) * jax.nn.gelu(gate)
    q = rope(rmsnorm(q.reshape(B, T, N_HEADS, HEAD_DIM), q_norm), pos)
    k = rope(rmsnorm(k.reshape(B, T, N_KV_HEADS, HEAD_DIM), k_norm), pos)
    v = v.reshape(B, T, N_KV_HEADS, HEAD_DIM)
    qi = rope(qi.reshape(B, T, N_IDX_HEADS, IDX_DIM), pos)
    ki = rope(ki.reshape(B, T, 1, IDX_DIM), pos)[:, :, 0]
    wi = wi * (N_IDX_HEADS ** -0.5 * IDX_DIM ** -0.5)
    if k_past is None:
        k_all, v_all, ki_all, k_pos = k, v, ki, pos
    else:
        P = k_past.shape[1]
        k_all = jnp.concatenate([k_past.astype(k.dtype), k], axis=1)
        v_all = jnp.concatenate([v_past.astype(v.dtype), v], axis=1)
        ki_all = jnp.concatenate([ki_past.astype(ki.dtype), ki], axis=1)
        k_pos = jnp.concatenate([jnp.arange(P, dtype=jnp.int32), pos])
    topk = min(TOPK_MAX, k_all.shape[1] // 4)
    y_b = dsa_attend(q, qi, wi, pos, k_all, v_all, ki_all, k_pos, topk)
    x = x + jnp.concatenate([y_a, y_b], axis=-1) @ w_out
    hf = rmsnorm(x, norm_ffn)
    gu = hf @ w_ffn_in
    x = x + (jax.nn.silu(gu[..., :D_FF]) * gu[..., D_FF:]) @ w_ffn_out
    return x, k, v, ki, h_last.astype(x.dtype), conv_new


def setup_inputs(seed: int = 0) -> dict:
    key = jax.random.key(seed)
    ks = jax.random.split(key, 24)
    f32 = jnp.float32
    nrm = lambda kk, shape, s: jax.random.normal(kk, shape, f32) * s
    a8 = jax.random.uniform(ks[10], (LRU_WIDTH,), f32, 0.9, 0.999)
    a_base = a8 ** (1.0 / LRU_C)
    return {
        "x_prompt": nrm(ks[0], (BATCH, SEQ, D_MODEL), 1.0),
        "x_sample": nrm(ks[1], (DEC_BATCH, DEC_SEQ, D_MODEL), 1.0),
        "cache_k": nrm(ks[2], (DEC_BATCH, PAST_LEN, N_KV_HEADS, HEAD_DIM), 1.0),
        "cache_v": nrm(ks[3], (DEC_BATCH, PAST_LEN, N_KV_HEADS, HEAD_DIM), 1.0),
        "cache_kidx": nrm(ks[4], (DEC_BATCH, PAST_LEN, IDX_DIM), 1.0),
        "state_h": nrm(ks[5], (DEC_BATCH, LRU_WIDTH), 0.5),
        "state_conv": nrm(ks[6], (DEC_BATCH, CONV_W - 1, LRU_WIDTH), 1.0),
        "norm_mix": 1.0 + nrm(ks[7], (D_MODEL,), 0.01),
        "w_in": nrm(ks[8], (D_MODEL, D_IN), D_MODEL ** -0.5),
        "conv_w": nrm(ks[9], (CONV_W, LRU_WIDTH), CONV_W ** -0.5),
        "conv_b": nrm(ks[11], (LRU_WIDTH,), 0.01),
        "w_rg": nrm(ks[12], (LRU_BLOCKS, LRU_BLOCK_W, LRU_BLOCK_W), LRU_BLOCK_W ** -0.5),
        "b_rg": nrm(ks[13], (LRU_BLOCKS, LRU_BLOCK_W), 0.01),
        "w_ig": nrm(ks[14], (LRU_BLOCKS, LRU_BLOCK_W, LRU_BLOCK_W), LRU_BLOCK_W ** -0.5),
        "b_ig": nrm(ks[15], (LRU_BLOCKS, LRU_BLOCK_W), 0.01),
        "lru_lambda": jnp.log(a_base) - jnp.log1p(-a_base),
        "q_norm": 1.0 + nrm(ks[16], (HEAD_DIM,), 0.01),
        "k_norm": 1.0 + nrm(ks[17], (HEAD_DIM,), 0.01),
        "w_out": nrm(ks[18], (LRU_WIDTH + ATTN_WIDTH, D_MODEL), (LRU_WIDTH + ATTN_WIDTH) ** -0.5),
        "norm_ffn": 1.0 + nrm(ks[19], (D_MODEL,), 0.01),
        "w_ffn_in": nrm(ks[20], (D_MODEL, 2 * D_FF), D_MODEL ** -0.5),
        "w_ffn_out": nrm(ks[21], (D_FF, D_MODEL), D_FF ** -0.5),
    }


def reference(x_prompt, x_sample, cache_k, cache_v, cache_kidx, state_h, state_conv,
              norm_mix, w_in, conv_w, conv_b, w_rg, b_rg, w_ig, b_ig, lru_lambda,
              q_norm, k_norm, w_out, norm_ffn, w_ffn_in, w_ffn_out):
    weights = (norm_mix, w_in, conv_w, conv_b, w_rg, b_rg, w_ig, b_ig, lru_lambda,
               q_norm, k_norm, w_out, norm_ffn, w_ffn_in, w_ffn_out)
    Bp, Tp, _ = x_prompt.shape
    Bs, Ts, _ = x_sample.shape
    P = cache_k.shape[1]
    pos_p = jnp.arange(Tp, dtype=jnp.int32)
    pos_s = P + jnp.arange(Ts, dtype=jnp.int32)
    yp = x_prompt
    conv0 = jnp.zeros((Bp, CONV_W - 1, LRU_WIDTH), x_prompt.dtype)
    h0 = jnp.zeros((Bp, LRU_WIDTH), x_prompt.dtype)
    ys = x_sample
    for _ in range(DEPTH):
        yp, k_p, v_p, ki_p, h_p, conv_p = layer(yp, pos_p, conv0, h0, None, None, None, *weights)
        ys, k_s, v_s, ki_s, h_s, conv_s = layer(ys, pos_s, state_conv, state_h, cache_k, cache_v, cache_kidx, *weights)
    return (yp, ys, k_p, v_p, ki_p, h_p, conv_p, k_s, v_s, ki_s, h_s, conv_s)
```

```python
import numpy as np
from contextlib import ExitStack
import concourse.bass as bass
import concourse.mybir as mybir
from concourse.bass_utils import run_bass_kernel_spmd

F32 = mybir.dt.float32
BF16 = mybir.dt.bfloat16
U8 = mybir.dt.uint8
FP8 = mybir.dt.float8e5
AF = mybir.ActivationFunctionType
ALU = mybir.AluOpType
AX = mybir.AxisListType

D = 1024
SEQ = 8192
PAST = 4096
DEC = 16
LW = 512
DFF = 2816
TOPK = 256
EPS = 1e-6
G = 2
BIS_DVE = 0.60
NIT = 20
NEG = -30000.0


class Buf:
    __slots__ = ("name", "w", "r")

    def __init__(self, name):
        self.name = name
        self.w = {}
        self.r = {}


class TL:
    def __init__(self, t, name, nb=0):
        self.t = t
        self.b = Buf(name)
        self.bb = [Buf("%s%d" % (name, i)) for i in range(nb)]


CENG = ("pe", "act", "dve", "pool")


class Sched:
    NDMA = 6

    def __init__(self, nc, stack):
        self.nc = nc
        self.streams = ("pe", "act", "dve", "pool", "sp")
        self.prog = {e: [] for e in self.streams}
        self.sems = {}
        self.ops = {e: [] for e in CENG}
        for e in CENG:
            self.sems[e] = stack.enter_context(nc.semaphore("s_" + e))
        self.dq = {}
        for q in ("sp", "pool"):
            for i in range(self.NDMA):
                self.sems["d_%s%d" % (q, i)] = stack.enter_context(nc.semaphore("d_%s%d" % (q, i)))
            self.dq[q] = 0
        self.waited = {e: {} for e in self.streams}
        self.nins = 0

    def _deps(self, engid, reads, writes):
        need = {}
        is_dma = engid.startswith("q_")

        def add(k, v, e, same_ok):
            if e == engid and not same_ok and engid == "pe":
                return
            if need.get(k, 0) < v:
                need[k] = v
        for b in reads:
            for k, (v, e) in b.w.items():
                add(k, v, e, True)
        for b in writes:
            for k, (v, e) in b.w.items():
                if is_dma and e.startswith("q_"):
                    continue
                add(k, v, e, False)
            for k, (v, e) in b.r.items():
                add(k, v, e, False)
        return need

    def _emit_waits(self, stream, need):
        for k, v in need.items():
            if self.waited[stream].get(k, 0) >= v:
                continue
            self.waited[stream][k] = v
            if k in self.ops:
                self.ops[k][v - 1][1] = True
            self.prog[stream].append(("wait", k, v))

    def _update(self, tok, reads, writes):
        k, v, e = tok
        for b in reads:
            if b.r.get(k, (0, None))[0] < v:
                b.r[k] = (v, e)
        for b in writes:
            if e.startswith("q_") and b.w and all(x[1].startswith("q_") for x in b.w.values()):
                b.w[k] = (v, e)
            else:
                b.w = {k: (v, e)}
            b.r = {}

    def op(self, eng, fn, reads=(), writes=()):
        need = self._deps(eng, reads, writes)
        self._emit_waits(eng, need)
        rec = [fn, False]
        self.ops[eng].append(rec)
        v = len(self.ops[eng])
        self.prog[eng].append(("op", rec))
        self._update((eng, v, eng), reads, writes)
        self.nins += 1

    def dma(self, q, out, in_, reads=(), writes=(), **kw):
        i = self.dq[q]
        self.dq[q] += 1
        slot = i % self.NDMA
        gen = i // self.NDMA + 1
        k = "d_%s%d" % (q, slot)
        need = self._deps("q_" + q, reads, writes)
        if gen > 1:
            need[k] = max(need.get(k, 0), 16 * (gen - 1))
        self._emit_waits(q, need)
        self.prog[q].append(("dma", k, out, in_, kw))
        self._update((k, 16 * gen, "q_" + q), reads, writes)
        self.nins += 1

    def finish(self):
        need = {}
        for q in ("sp", "pool"):
            n = self.dq[q]
            for slot in range(self.NDMA):
                c = (n - slot + self.NDMA - 1) // self.NDMA
                if c > 0:
                    need["d_%s%d" % (q, slot)] = 16 * c
        for e in CENG:
            if self.ops[e]:
                need[e] = len(self.ops[e])
        self._emit_waits("sp", need)

    def emit(self):
        nc = self.nc
        cum = {}
        for e in CENG:
            c = 0
            arr = []
            for rec in self.ops[e]:
                if rec[1]:
                    c += 1
                arr.append(c)
            cum[e] = arr
        sems = self.sems

        def run(stream, e):
            for it in self.prog[stream]:
                if it[0] == "wait":
                    _, k, v = it
                    val = cum[k][v - 1] if k in cum else v
                    e.wait_ge(sems[k], val)
                elif it[0] == "op":
                    rec = it[1]
                    ins = rec[0](e)
                    if rec[1]:
                        ins.then_inc(sems[stream], 1)
                else:
                    _, k, out, in_, kw = it
                    e.dma_start(out=out, in_=in_, **kw).then_inc(sems[k], 16)
        with nc.Block() as block:
            @block.sync
            def _(e):
                run("sp", e)

            @block.tensor
            def _(e):
                run("pe", e)

            @block.scalar
            def _(e):
                run("act", e)

            @block.vector
            def _(e):
                run("dve", e)

            @block.gpsimd
            def _(e):
                run("pool", e)


C_GMIX, C_GFFN, C_CW, C_CB, C_HBRG, C_HBIG, C_HNSP, C_PAR, C_NEG, C_LAM, C_TMP, C_MH = 0, 8, 16, 32, 36, 40, 44, 48, 49, 50, 54, 58
NCOLP = 64


class Ctx:
    pass


def interleave(fg, bgs, ratio=1):
    bgs = list(bgs)

    def bg_step():
        while bgs:
            try:
                next(bgs[0])
                return True
            except StopIteration:
                bgs.pop(0)
        return False
    for _ in fg:
        for _r in range(ratio):
            if not bg_step():
                break
    while bg_step():
        pass


def build(NB, with_sample=True):
    NOWN = NB // 2
    NGRP = NOWN // G
    NKB = max(NB, min(NB + 33, 64))
    SMP_OFF = NB if NB + 33 <= NKB else NKB - 33
    W = G * 128
    nc = bass.Bass("TRN2", target_bir_lowering=False)

    def din(name, shape):
        return nc.dram_tensor(name, list(shape), F32, kind="ExternalInput").ap()

    def dout(name, shape):
        return nc.dram_tensor(name, list(shape), F32, kind="ExternalOutput").ap()

    I = Ctx()
    I.x_all = din("x_all", [NB * 128, D])
    I.x_own = din("x_own", [NOWN * 128, D])
    I.x_smp = din("x_smp", [128, D])
    I.cache_k = din("cache_k", [PAST, 128])
    I.cache_v = din("cache_v", [PAST, 128])
    I.cache_ki = din("cache_ki", [PAST, 64])
    I.state_h = din("state_h", [LW])
    I.state_conv = din("state_conv", [3, LW])
    I.w_light = din("w_light", [D, 832])
    I.w_own = din("w_own", [3, 128, 8, 512])
    I.w_wi = din("w_wi", [D, 8])
    I.w_o = din("w_o", [2, 128, 8, 512])
    I.w_fi = din("w_fi", [11, 128, 8, 512])
    I.w_fo = din("w_fo", [2, 128, 22, 512])
    I.w_rg = din("w_rg", [8, 64, 64])
    I.w_ig = din("w_ig", [8, 64, 64])
    I.colsrc = din("colsrc", [128, NCOLP])
    I.gq = din("gq", [64])
    I.gk = din("gk", [64])
    I.ident = din("ident", [128, 128])
    I.pw = din("pw", [128, NIT])
    I.cos_all = din("cos_all", [NB * 128, 32])
    I.sin_all = din("sin_all", [NB * 128, 32])
    I.cos_own = din("cos_own", [NOWN * 128, 32])
    I.sin_own = din("sin_own", [NOWN * 128, 32])
    I.cos_smp = din("cos_smp", [128, 32])
    I.sin_smp = din("sin_smp", [128, 32])
    I.cm_own = din("cm_own", [128, 256])
    I.cm_smp = din("cm_smp", [128, 256])
    O = Ctx()
    O.y_own = dout("y_own", [NOWN * 128, D])
    O.kvk_all = dout("kvk_all", [NB * 128, 320])
    O.hc_p = dout("hc_p", [4, LW])
    O.y_smp = dout("y_smp", [128, D])
    O.kvk_smp = dout("kvk_smp", [128, 320])
    O.hc_s = dout("hc_s", [4, LW])

    with ExitStack() as st:
        S = Sched(nc, st)

        def T(name, shape, dt, nb=0):
            return TL(st.enter_context(nc.sbuf_tensor("sb_" + name, list(shape), dt)), name, nb)

        def P(name, shape, dt):
            return TL(st.enter_context(nc.psum_tensor(name, list(shape), dt)), name)

        C = Ctx()
        C.ident4 = T("ident4", [128, 4, 128], BF16)
        C.wl = T("wl", [128, 8, 832], BF16)
        C.wwi = T("wwi", [128, 8, 8], BF16)
        C.wrg = T("wrg", [128, 4, 128], BF16)
        C.wig = T("wig", [128, 4, 128], BF16)
        C.colp = T("colp", [128, NCOLP], F32)
        C.gq = T("gq", [128, 64], F32)
        C.gk = T("gk", [128, 64], F32)
        C.pw = T("pw", [128, NIT], F32)
        C.cm = T("cm", [128, 256], F32)
        C.kT = T("kT", [128, NKB * 128], BF16, NKB)
        C.kiT = T("kiT", [128, NKB * 128], BF16, NKB)
        C.Vc = T("Vc", [128, NKB, 2, 65], BF16, NKB)
        C.xt = T("xt", [128, D], F32)
        C.hb = T("hb", [128, D], BF16)
        C.hT = T("hT", [128, 8, 256], BF16)
        C.jk = T("jk", [128, 16], BF16)
        C.arena = T("arena", [128, NKB * 128], F32, (NKB + 3) // 4)
        C.L_xr = T("L_xr", [128, 4, 259], F32)
        C.L_xc = T("L_xc", [128, 4, 256], F32)
        C.L_r = T("L_r", [128, 4, 256], F32)
        C.L_i = T("L_i", [128, 4, 256], F32)
        C.xcb = T("xcb", [128, 4, 256], BF16)
        C.xtail = T("xtail", [128, 4, 3], F32)
        C.hstate = T("hstate", [128, 4], F32)
        C.kv = T("kv", [128, 320], F32)
        C.kout = T("kout", [128, 320], F32)
        C.rp = [T("rp%d" % i, [128, 8, 32], F32) for i in range(3)]
        C.kb16 = T("kb16", [128, 256], BF16)
        C.cs = T("cs", [128, 2, 32], F32)
        C.stats = [T("stat%d" % i, [128, 24], F32) for i in range(8)]
        C.stat_i = 0
        C.mhalf = T("mhalf", [128, 8], F32)
        C.xg = T("xg", [128, G, D], F32)
        C.hTg = T("hTg", [128, 8, W], BF16)
        C.sgT = T("sgT", [128, 4, W], F32)
        C.qTg = [T("qTg%d" % i, [128, 8, W], BF16) for i in range(2)]
        C.qiTg = T("qiTg", [128, 4, W], BF16)
        C.w8g = T("w8g", [128, G, 8], F32)
        C.yTg = [T("yTg%d" % i, [128, 8, W], BF16) for i in range(2)]
        C.t16 = T("t16", [128, 512], BF16)
        C.cso = T("cso", [128, 2, 32], F32)
        C.rl = [T("rl%d" % i, [128, 512], F32) for i in range(2)]
        C.mball = T("mball", [128, NKB * 128], FP8)
        C.pT = [T("pT%d" % i, [128, 512], BF16) for i in range(4)]
        C.bis = T("bis", [128, 12 + NIT], F32)
        C.bisA = T("bisA", [128, 2], F32)
        C.bisP = T("bisP", [128, 2], F32)
        C.bmid = T("bmid", [128, 2], F32)
        C.bcnt = T("bcnt", [128, 2], F32)
        C.bg = T("bg", [128, 2], F32)
        C.onesf = T("onesf", [128, 64], F32)
        C.rc = T("rc", [128, 8], F32)
        C.actT = T("actT", [128, 4, W], BF16)
        C.ft = [T("ft%d" % i, [128, 512], F32) for i in range(2)]
        C.wb = [T("wb%d" % i, [128, 8, 512], BF16) for i in range(2)]
        C.wb_i = 0
        C.ps_mm = [P("ps_mm%d" % i, [128, 512], F32) for i in range(2)]
        C.ps_tr = P("ps_tr", [128, 8, 128], BF16)
        C.ps_s = [P("ps_s%d" % i, [128, 512], F32) for i in range(2)]
        C.ps_pv = [P("ps_pv%d" % i, [128, 512], F32) for i in range(2)]
        C.pvv = [p_.t[:, 0:260].rearrange("p (h d) -> p h d", h=4) for p_ in C.ps_pv]
        C.ps_g = P("ps_g", [128, 512], F32)
        ident = C.ident4.t[:, 0, :]
        IDB = C.ident4.b

        def stat():
            s = C.stats[C.stat_i % 8]
            C.stat_i += 1
            return s

        def next_wb():
            w = C.wb[C.wb_i % 2]
            C.wb_i += 1
            return w

        colp = C.colp.t

        def col(i):
            return colp[:, i:i + 1]

        def mm(out, lhsT, rhs, start, stop, reads, writes):
            S.op("pe", lambda e: e.matmul(out=out, lhsT=lhsT, rhs=rhs, start=start, stop=stop,
                                          skip_group_check=True), reads=reads, writes=writes)

        def tr(out, in_, reads, writes):
            S.op("pe", lambda e: e.transpose(out=out, in_=in_, identity=ident), reads=list(reads) + [IDB], writes=writes)

        def act(out, in_, func, reads, writes, **kw):
            S.op("act", lambda e: e.activation(out=out, in_=in_, func=func, **kw), reads=reads, writes=writes)

        def tt(eng, out, in0, in1, op, reads, writes):
            S.op(eng, lambda e: e.tensor_tensor(out=out, in0=in0, in1=in1, op=op), reads=reads, writes=writes)

        def ts(eng, out, in0, s1, s2, op0, op1, reads, writes, **kw):
            if op1 is None:
                S.op(eng, lambda e: e.tensor_scalar(out=out, in0=in0, scalar1=s1, scalar2=None, op0=op0, **kw),
                     reads=reads, writes=writes)
            else:
                S.op(eng, lambda e: e.tensor_scalar(out=out, in0=in0, scalar1=s1, scalar2=s2, op0=op0, op1=op1, **kw),
                     reads=reads, writes=writes)

        def stt(out, in0, scalar, in1, op0, op1, reads, writes):
            S.op("dve", lambda e: e.scalar_tensor_tensor(out=out, in0=in0, scalar=scalar, in1=in1, op0=op0, op1=op1),
                 reads=reads, writes=writes)

        def cp(eng, out, in_, reads, writes):
            if eng == "act":
                act(out, in_, AF.Identity, reads, writes)
            else:
                S.op(eng, lambda e: e.tensor_copy(out=out, in_=in_), reads=reads, writes=writes)

        def memset(eng, ap, val, writes):
            S.op(eng, lambda e: e.memset(ap, val), writes=writes)

        for r in range(4):
            S.dma("pool", C.ident4.t[:, r, :], I.ident, writes=[C.ident4.b])
        S.dma("pool", C.wl.t[:], I.w_light.rearrange("(k p) c -> p k c", p=128), writes=[C.wl.b])
        S.dma("pool", C.wwi.t[:], I.w_wi.rearrange("(k p) c -> p k c", p=128), writes=[C.wwi.b])
        memset("pool", C.wrg.t[:], 0.0, [C.wrg.b])
        memset("pool", C.wig.t[:], 0.0, [C.wig.b])
        for n in range(8):
            q, hf = n // 2, n % 2
            S.dma("pool", C.wrg.t[64 * hf:64 * hf + 64, q, 64 * hf:64 * hf + 64], I.w_rg[n], reads=[], writes=[C.wrg.b])
            S.dma("pool", C.wig.t[64 * hf:64 * hf + 64, q, 64 * hf:64 * hf + 64], I.w_ig[n], reads=[], writes=[C.wig.b])
        S.dma("sp", C.colp.t[:], I.colsrc, writes=[C.colp.b])
        S.dma("sp", C.gq.t[:], I.gq.partition_broadcast(128), writes=[C.gq.b])
        S.dma("sp", C.gk.t[:], I.gk.partition_broadcast(128), writes=[C.gk.b])
        S.dma("sp", C.pw.t[:], I.pw, writes=[C.pw.b])
        memset("pool", C.mhalf.t[:], -0.5, [C.mhalf.b])
        memset("pool", C.onesf.t[:], 1.0, [C.onesf.b])
        memset("pool", C.qTg[0].t[:], 0.0, [C.qTg[0].b])
        memset("pool", C.qTg[1].t[:], 0.0, [C.qTg[1].b])
        memset("pool", C.Vc.t[:], 1.0, [C.Vc.b] + C.Vc.bb)
        act(colp[:, C_TMP:C_TMP + 4], colp[:, C_LAM:C_LAM + 4], AF.Exp, [C.colp.b], [C.colp.b], scale=-1.0)
        act(colp[:, C_TMP:C_TMP + 4], colp[:, C_TMP:C_TMP + 4], AF.Ln, [C.colp.b], [C.colp.b], bias=1.0)
        ts("dve", colp[:, C_HNSP:C_HNSP + 4], colp[:, C_TMP:C_TMP + 4], -4.0, None, ALU.mult, None, [C.colp.b], [C.colp.b])
        ts("dve", colp[:, C_HBRG:C_HBRG + 8], colp[:, C_HBRG:C_HBRG + 8], 0.5, None, ALU.mult, None, [C.colp.b], [C.colp.b])

        def rstd_cols(s, n, inv_n):
            ts("dve", s.t[:, 16:16 + n], s.t[:, 0:n], inv_n, EPS, ALU.mult, ALU.add, [s.b], [s.b])
            tt("pool", s.t[:, 8:8 + n], s.t[:, 16:16 + n], C.mhalf.t[:, 0:n], ALU.pow, [s.b, C.mhalf.b], [s.b])

        def norm_transpose(src, src_b, gcol, dstT, tokoff):
            s = stat()
            act(C.jk.t[:, 0:1].to_broadcast([128, D]), src, AF.Square, [src_b], [s.b], accum_out=s.t[:, 0:1])
            rstd_cols(s, 1, 1.0 / D)
            ts("dve", C.hb.t[:], src, s.t[:, 8:9], None, ALU.mult, None, [src_b, s.b], [C.hb.b])
            for j in range(8):
                tr(C.ps_tr.t[:, j, :], C.hb.t[:, j * 128:(j + 1) * 128], [C.hb.b], [C.ps_tr.b])
            for j in range(8):
                act(dstT.t[:, j, tokoff:tokoff + 128], C.ps_tr.t[:, j, :], AF.Identity,
                    [C.ps_tr.b, C.colp.b], [dstT.b], scale=col(gcol + j))

        def rope(eng, src, dst, H, cos, sin, reads, writes):
            cb = cos.unsqueeze(1).to_broadcast([128, H, 32])
            sb = sin.unsqueeze(1).to_broadcast([128, H, 32])
            x1, x2 = src[:, :, 0:32], src[:, :, 32:64]
            t = [C.rp[i].t[:, 0:H, :] for i in range(3)]
            tb = [C.rp[i].b for i in range(3)]
            tt(eng, t[0], x1, cb, ALU.mult, reads, [tb[0]])
            tt(eng, t[1], x2, sb, ALU.mult, reads, [tb[1]])
            tt(eng, t[2], x1, sb, ALU.mult, reads, [tb[2]])
            tt(eng, dst[:, :, 0:32], t[0], t[1], ALU.subtract, [tb[0], tb[1]] + reads, writes)
            tt(eng, t[0], x2, cb, ALU.mult, reads, [tb[0]])
            tt(eng, dst[:, :, 32:64], t[0], t[2], ALU.add, [tb[0], tb[2]] + reads, writes)

        def head_rmsnorm(eng, src, H, gain, src_b):
            s = stat()
            sq = C.ft[1]
            flat = src.rearrange("p h d -> p (h d)")
            act(sq.t[:, 0:H * 64], flat, AF.Square, [src_b], [sq.b])
            S.op("dve", lambda e: e.tensor_reduce(out=s.t[:, 0:H], in_=sq.t[:, 0:H * 64].rearrange("p (h d) -> p h d", h=H),
                                                  axis=AX.X, op=ALU.add), reads=[sq.b], writes=[s.b])
            rstd_cols(s, H, 1.0 / 64)
            tt(eng, src, src, s.t[:, 8:8 + H].unsqueeze(2).to_broadcast([128, H, 64]), ALU.mult, [src_b, s.b], [src_b])
            tt(eng, src, src, gain.t[:, :].unsqueeze(1).to_broadcast([128, H, 64]), ALU.mult, [src_b, gain.b], [src_b])

        def light(x_src, nblk, blk0, cos_src, sin_src, first, kvk_out, treal):
            Tn = 128 * nblk
            XR, XC, RR, II = C.L_xr, C.L_xc, C.L_r, C.L_i
            for n in range(nblk):
                S.dma("sp", C.xt.t[:], x_src[n * 128:(n + 1) * 128, :], writes=[C.xt.b])
                norm_transpose(C.xt.t[:], C.xt.b, C_GMIX, C.hT, n * 128)
                yield
            cp("pool", XR.t[:, :, 0:3], C.xtail.t[:, :, :], [C.xtail.b], [XR.b])
            for half in range(2):
                ps = C.ps_mm[0]
                first_mm = True
                for qq in range(2):
                    q = half * 2 + qq
                    for k in range(8):
                        mm(ps.t[:, qq * Tn:(qq + 1) * Tn], C.wl.t[:, k, q * 128:(q + 1) * 128], C.hT.t[:, k, 0:Tn],
                           first_mm, k == 7, [C.wl.b, C.hT.b], [ps.b])
                        first_mm = False
                act(XR.t[:, 2 * half:2 * half + 2, 3:3 + Tn], ps.t[:, 0:2 * Tn].rearrange("p (a b) -> p a b", a=2),
                    AF.Identity, [ps.b], [XR.b])
                yield
            for n in range(nblk):
                S.dma("sp", C.cs.t[:, 0, :], cos_src[n * 128:(n + 1) * 128, :], writes=[C.cs.b])
                S.dma("sp", C.cs.t[:, 1, :], sin_src[n * 128:(n + 1) * 128, :], writes=[C.cs.b])
                ps = C.ps_mm[0]
                for k in range(8):
                    mm(ps.t[:, 0:320], C.hT.t[:, k, n * 128:(n + 1) * 128], C.wl.t[:, k, 512:832], k == 0, k == 7,
                       [C.wl.b, C.hT.b], [ps.b])
                act(C.kv.t[:], ps.t[:, 0:320], AF.Identity, [ps.b], [C.kv.b])
                kk = C.kv.t[:, 0:128].rearrange("p (h d) -> p h d", h=2)
                head_rmsnorm("dve", kk, 2, C.gk, C.kv.b)
                rope("dve", kk, C.kout.t[:, 0:128].rearrange("p (h d) -> p h d", h=2), 2,
                     C.cs.t[:, 0, :], C.cs.t[:, 1, :], [C.kv.b, C.cs.b], [C.kout.b])
                rope("dve", C.kv.t[:, 256:320].rearrange("p (h d) -> p h d", h=1),
                     C.kout.t[:, 256:320].rearrange("p (h d) -> p h d", h=1), 1,
                     C.cs.t[:, 0, :], C.cs.t[:, 1, :], [C.kv.b, C.cs.b], [C.kout.b])
                cp("pool", C.kout.t[:, 128:256], C.kv.t[:, 128:256], [C.kv.b], [C.kout.b])
                S.dma("sp", kvk_out[n * 128:(n + 1) * 128, :], C.kout.t[:], reads=[C.kout.b], writes=[])
                blk = blk0 + n
                cp("act", C.kb16.t[:, 0:128], C.kout.t[:, 0:128], [C.kout.b], [C.kb16.b])
                cp("act", C.kb16.t[:, 128:192], C.kout.t[:, 256:320], [C.kout.b], [C.kb16.b])
                cp("act", C.kb16.t[:, 192:256], C.kout.t[:, 256:320], [C.kout.b], [C.kb16.b])
                cp("pool", C.Vc.t[:, blk, :, 0:64], C.kv.t[:, 128:256].rearrange("p (h d) -> p h d", h=2),
                   [C.kv.b], [C.Vc.bb[blk]])
                for j in range(2):
                    tr(C.ps_tr.t[:, j, :], C.kb16.t[:, j * 128:(j + 1) * 128], [C.kb16.b], [C.ps_tr.b])
                cp("act", C.kT.t[:, blk * 128:(blk + 1) * 128], C.ps_tr.t[:, 0, :], [C.ps_tr.b], [C.kT.bb[blk]])
                cp("act", C.kiT.t[:, blk * 128:(blk + 1) * 128], C.ps_tr.t[:, 1, :], [C.ps_tr.b], [C.kiT.bb[blk]])
                yield
            for q in range(4):
                ts("dve", XC.t[:, q, 0:Tn], XR.t[:, q, 0:Tn], col(C_CW + q), col(C_CB + q), ALU.mult, ALU.add,
                   [XR.b, C.colp.b], [XC.b])
                for j in range(1, 4):
                    stt(XC.t[:, q, 0:Tn], XR.t[:, q, j:j + Tn], col(C_CW + 4 * j + q), XC.t[:, q, 0:Tn], ALU.mult, ALU.add,
                        [XR.b, XC.b, C.colp.b], [XC.b])
            cp("pool", C.xtail.t[:, :, :], XR.t[:, :, treal:treal + 3], [XR.b], [C.xtail.b])
            cp("pool", C.xcb.t[:, :, 0:Tn], XC.t[:, :, 0:Tn], [XC.b], [C.xcb.b])
            yield
            for (wbd, hbcol, dst) in ((C.wrg, C_HBRG, RR), (C.wig, C_HBIG, II)):
                for half in range(2):
                    first_mm = True
                    for qq in range(2):
                        q = 2 * half + qq
                        mm(C.ps_mm[0].t[:, qq * Tn:(qq + 1) * Tn], wbd.t[:, q, :], C.xcb.t[:, q, 0:Tn], first_mm, True,
                           [wbd.b, C.xcb.b], [C.ps_mm[0].b])
                        first_mm = False
                    for qq in range(2):
                        q = 2 * half + qq
                        act(dst.t[:, q, 0:Tn], C.ps_mm[0].t[:, qq * Tn:(qq + 1) * Tn], AF.Tanh, [C.ps_mm[0].b, C.colp.b], [dst.b],
                            scale=0.5, bias=col(hbcol + q))
                yield
            for q in range(4):
                act(RR.t[:, q, 0:Tn], RR.t[:, q, 0:Tn], AF.Exp, [RR.b, C.colp.b], [RR.b], scale=col(C_HNSP + q), bias=col(C_HNSP + q))
            M = XR.t[:, :, 0:Tn]
            tt("pool", M, RR.t[:, :, 0:Tn], RR.t[:, :, 0:Tn], ALU.mult, [RR.b], [XR.b])
            act(M, M, AF.Ln, [XR.b], [XR.b], scale=-1.0, bias=1.0)
            act(M, M, AF.Exp, [XR.b], [XR.b], scale=0.5)
            if first:
                memset("pool", XR.t[:, :, 0:1], 1.0, [XR.b])
            yield
            stt(II.t[:, :, 0:Tn], II.t[:, :, 0:Tn], 1.0, XC.t[:, :, 0:Tn], ALU.add, ALU.mult, [II.b, XC.b], [II.b])
            stt(II.t[:, :, 0:Tn], II.t[:, :, 0:Tn], 0.5, M, ALU.mult, ALU.mult, [II.b, XR.b], [II.b])
            for q in range(4):
                S.op("dve", lambda e, q=q: e.tensor_tensor_scan(out=XC.t[:, q, 0:Tn], data0=RR.t[:, q, 0:Tn],
                                                               data1=II.t[:, q, 0:Tn], initial=C.hstate.t[:, q:q + 1],
                                                               op0=ALU.mult, op1=ALU.add),
                     reads=[RR.b, II.b, C.hstate.b], writes=[XC.b])
            cp("pool", C.hstate.t[:, :], XC.t[:, :, treal - 1], [XC.b], [C.hstate.b])
            yield

        def make_ya(gi, sample, yT):
            HS = C.L_xc
            if sample:
                hsel = HS.t[:, :, 0:128]
            else:
                tt("pool", HS.t[:, :, 128:256], HS.t[:, :, 128:256], HS.t[:, :, 0:128], ALU.subtract, [HS.b], [HS.b])
                stt(HS.t[:, :, 128:256], HS.t[:, :, 128:256], col(C_PAR), HS.t[:, :, 0:128], ALU.mult, ALU.add,
                    [HS.b, C.colp.b], [HS.b])
                hsel = HS.t[:, :, 128:256]
            stt(yT.t[:, 0:4, gi * 128:(gi + 1) * 128], hsel, 0.5, C.sgT.t[:, :, gi * 128:(gi + 1) * 128],
                ALU.mult, ALU.mult, [HS.b, C.sgT.b], [yT.b])

        def stream(src_ap, nk=8):
            w = next_wb()
            S.dma("pool", w.t[:, 0:nk, :], src_ap, writes=[w.b])
            return w

        def own_prefetch(x_src, ng):
            S.dma("sp", C.xg.t[:, 0:ng, :], x_src.rearrange("(n p) d -> p n d", p=128), writes=[C.xg.b])
            return [stream(I.w_own[2]), stream(I.w_own[0])]

        def own_proj(x_src, ng, cos_src, sin_src, qT, pre=None, parts=("norm", "wi", "qi", "gate", "q"), bank0=False):
            xg = C.xg
            Wn = ng * 128
            qf = C.ft[0]
            if "norm" in parts:
                if pre is None:
                    S.dma("sp", xg.t[:, 0:ng, :], x_src.rearrange("(n p) d -> p n d", p=128), writes=[xg.b])
                for gi in range(ng):
                    norm_transpose(xg.t[:, gi, :], xg.b, C_GMIX, C.hTg, gi * 128)
            if "wi" in parts:
                for gi in range(ng):
                    for k in range(8):
                        mm(C.ps_g.t[:, 0:8], C.hTg.t[:, k, gi * 128:(gi + 1) * 128], C.wwi.t[:, k, :], k == 0, k == 7,
                           [C.wwi.b, C.hTg.b], [C.ps_g.b])
                    act(C.w8g.t[:, gi, :], C.ps_g.t[:, 0:8], AF.Identity, [C.ps_g.b], [C.w8g.b],
                        scale=float((8 ** -0.5) * (64 ** -0.5)))

            def qpiece(piece, w):
                for gi in range(ng):
                    ps = C.ps_mm[0] if bank0 else C.ps_mm[gi % 2]
                    for k in range(8):
                        mm(ps.t[:, 0:512], C.hTg.t[:, k, gi * 128:(gi + 1) * 128], w.t[:, k, :], k == 0, k == 7,
                           [w.b, C.hTg.b], [ps.b])
                    act(qf.t[:], ps.t[:, 0:512], AF.Identity, [ps.b], [qf.b])
                    q3 = qf.t[:].rearrange("p (h d) -> p h d", h=8)
                    S.dma("sp", C.cso.t[:, 0, :], cos_src[gi * 128:(gi + 1) * 128, :], writes=[C.cso.b])
                    S.dma("sp", C.cso.t[:, 1, :], sin_src[gi * 128:(gi + 1) * 128, :], writes=[C.cso.b])
                    if piece == 1:
                        head_rmsnorm("dve", q3, 8, C.gq, qf.b)
                    rope("dve", q3, q3, 8, C.cso.t[:, 0, :], C.cso.t[:, 1, :], [qf.b, C.cso.b], [qf.b])
                    cp("act", C.t16.t[:], qf.t[:], [qf.b], [C.t16.b])
                    for j in range(4):
                        tr(C.ps_tr.t[:, j, :], C.t16.t[:, j * 128:(j + 1) * 128], [C.t16.b], [C.ps_tr.b])
                    if piece == 1:
                        cp("act", qT.t[0:64, 0:4, gi * 128:(gi + 1) * 128], C.ps_tr.t[0:64, 0:4, :], [C.ps_tr.b], [qT.b])
                        cp("act", qT.t[64:128, 4:8, gi * 128:(gi + 1) * 128], C.ps_tr.t[64:128, 0:4, :], [C.ps_tr.b], [qT.b])
                    else:
                        cp("act", C.qiTg.t[:, :, gi * 128:(gi + 1) * 128], C.ps_tr.t[:, 0:4, :], [C.ps_tr.b], [C.qiTg.b])
            if "qi" in parts:
                qpiece(2, pre[0] if pre is not None else stream(I.w_own[2]))
            if "gate" in parts:
                w = pre[1] if pre is not None else stream(I.w_own[0])
                for half in range(2):
                    ps = C.ps_mm[0] if bank0 else C.ps_mm[half]
                    first_mm = True
                    for qq in range(2):
                        q = half * 2 + qq
                        for k in range(8):
                            mm(ps.t[:, qq * Wn:(qq + 1) * Wn], w.t[:, k, q * 128:(q + 1) * 128], C.hTg.t[:, k, 0:Wn],
                               first_mm, k == 7, [w.b, C.hTg.b], [ps.b])
                            first_mm = False
                    f0, f1 = C.ft[0], C.ft[1]
                    pv = ps.t[:, 0:2 * Wn]
                    act(f0.t[:, 0:2 * Wn], pv, AF.Square, [ps.b], [f0.b])
                    ts("dve", f0.t[:, 0:2 * Wn], f0.t[:, 0:2 * Wn], 0.044715, 1.0, ALU.mult, ALU.add, [f0.b], [f0.b])
                    tt("dve", f0.t[:, 0:2 * Wn], f0.t[:, 0:2 * Wn], pv, ALU.mult, [f0.b, ps.b], [f0.b])
                    act(f1.t[:, 0:2 * Wn], f0.t[:, 0:2 * Wn], AF.Tanh, [f0.b], [f1.b], scale=0.7978845608028654)
                    stt(C.sgT.t[:, 2 * half:2 * half + 2, 0:Wn], f1.t[:, 0:2 * Wn].rearrange("p (a b) -> p a b", a=2), 1.0,
                        pv.rearrange("p (a b) -> p a b", a=2), ALU.add, ALU.mult, [f1.b, ps.b], [C.sgT.b])
            if "q" in parts:
                qpiece(1, stream(I.w_own[1]))

        def indexer(gi, nkb, kb0=0):
            qs = slice(gi * 128, (gi + 1) * 128)
            ngrp = (nkb + 3) // 4
            sc = C.arena.t
            for g in range(ngrp):
                nb_ = min(4, nkb - 4 * g)
                N = nb_ * 128
                c0 = g * 512
                kib = [C.kiT.bb[kb0 + 4 * g + i] for i in range(nb_)]
                k0 = kb0 * 128 + c0
                scb = C.arena.bb[g]
                for h in range(8):
                    j, hf = h // 2, h % 2
                    pr = slice(64 * hf, 64 * hf + 64)
                    ps = (C.ps_mm[1], C.ps_g, C.ps_pv[0], C.ps_pv[1])[h % 4]
                    mm(ps.t[:, 0:N], C.qiTg.t[pr, j, qs], C.kiT.t[pr, k0:k0 + N], True, True, [C.qiTg.b] + kib, [ps.b])
                    rl = C.rl[h % 2]
                    act(rl.t[:, 0:N], ps.t[:, 0:N], AF.Relu, [ps.b], [rl.b])
                    if h == 0:
                        ts("dve", sc[:, c0:c0 + N], rl.t[:, 0:N], C.w8g.t[:, gi, 0:1], None, ALU.mult, None,
                           [rl.b, C.w8g.b], [scb])
                    else:
                        stt(sc[:, c0:c0 + N], rl.t[:, 0:N], C.w8g.t[:, gi, h:h + 1], sc[:, c0:c0 + N], ALU.mult, ALU.add,
                            [rl.b, C.w8g.b, scb], [scb])
                    yield

        def bisect(nkb, cmask_b):
            S_all = nkb * 128
            ngrp = (nkb + 3) // 4
            sc = C.arena.t
            SB = C.arena.bb[0:ngrp]
            B = C.bis.t
            bb = C.bis.b
            A = C.bisA.t
            ab = C.bisA.b
            nD = max(128, int(round(BIS_DVE * nkb)) * 128)
            nA = S_all - nD
            S.op("dve", lambda e: e.tensor_reduce(out=B[:, 0:1], in_=sc[:, 0:S_all], axis=AX.X, op=ALU.max),
                 reads=SB, writes=[bb])
            S.op("dve", lambda e: e.tensor_reduce(out=B[:, 1:2], in_=sc[:, 0:S_all], axis=AX.X, op=ALU.min),
                 reads=SB, writes=[bb])
            lastg = [C.arena.bb[g] for g in sorted(set([(nkb - 2) // 4, (nkb - 1) // 4]))]
            tt("dve", sc[:, S_all - 256:S_all], sc[:, S_all - 256:S_all], C.cm.t[:], ALU.add, lastg + [C.cm.b], lastg)
            yield
            tt("dve", B[:, 2:3], B[:, 0:1], B[:, 1:2], ALU.subtract, [bb], [bb])
            ts("dve", B[:, 3:4], B[:, 2:3], 1.0 + 2.0 / 64, 2e-12, ALU.mult, ALU.add, [bb], [bb])
            ts("dve", B[:, 4:5], B[:, 2:3], -1.0 / 64, -1e-12, ALU.mult, ALU.add, [bb], [bb])
            tt("dve", B[:, 4:5], B[:, 4:5], B[:, 1:2], ALU.add, [bb], [bb])
            ts("dve", B[:, 12:12 + NIT], C.pw.t[:], B[:, 3:4], None, ALU.mult, None, [bb, C.pw.b], [bb])
            thr = TOPK - 0.5 - 0.5 * nA
            MID, CNT, GG = C.bmid, C.bcnt, C.bg
            tt("dve", MID.t[:, 0:1], B[:, 12:13], B[:, 4:5], ALU.add, [bb], [MID.b])
            for it in range(NIT):
                last = it == NIT - 1
                if nA > 0:
                    act(C.jk.t[:, 2:3].to_broadcast([128, nA]), sc[:, nD:S_all], AF.Sign, SB + [MID.b], [ab],
                        scale=-1.0, bias=MID.t[:, 0:1], accum_out=A[:, 0:1])
                    act(A[:, 1:2], A[:, 0:1], AF.Identity, [ab], [ab], scale=0.5, bias=float(thr))
                else:
                    memset("pool", A[:, 1:2], float(thr), [ab])
                S.op("dve", lambda e: e.tensor_scalar(out=C.jk.t[:, 4:5].to_broadcast([128, nD]), in0=sc[:, 0:nD], scalar1=MID.t[:, 0:1],
                                                      scalar2=0.0, op0=ALU.is_ge, op1=ALU.add, accum_out=CNT.t[:, 0:1]),
                     reads=SB + [MID.b], writes=[CNT.b])
                sn = B[:, 12 + it + 1:13 + it + 1] if not last else B[:, 12 + it:13 + it]
                tt("pool", C.bisP.t[:, 0:1], MID.t[:, 0:1], sn, ALU.subtract, [MID.b, bb], [C.bisP.b])
                ts("dve", GG.t[:, 0:1], CNT.t[:, 0:1], A[:, 1:2], sn, ALU.is_ge, ALU.mult, [CNT.b, bb, ab], [GG.b])
                if not last:
                    stt(MID.t[:, 0:1], GG.t[:, 0:1], 2.0, C.bisP.t[:, 0:1], ALU.mult, ALU.add, [GG.b, C.bisP.b], [MID.b])
                else:
                    tt("dve", B[:, 4:5], GG.t[:, 0:1], C.bisP.t[:, 0:1], ALU.add, [GG.b, C.bisP.b], [bb])
                yield

        def maskgen(nkb):
            sc = C.arena.t
            B = C.bis.t
            ngrp = (nkb + 3) // 4
            for g in range(ngrp):
                c0 = g * 512
                N = min(4, nkb - 4 * g) * 128
                ts("dve", C.mball.t[:, c0:c0 + N], sc[:, c0:c0 + N], B[:, 4:5], col(C_NEG), ALU.is_lt, ALU.mult,
                   [C.arena.bb[g], C.bis.b, C.colp.b], [C.mball.b], saturate=False)

        def attention(gi, nkb, yT, qT, kb0=0):
            qs = slice(gi * 128, (gi + 1) * 128)
            sc = C.arena.t
            B = C.bis.t
            bb = C.bis.b

            def sbank(kb, half):
                return C.ps_s[half] if kb % 2 == 0 else (C.ps_mm[1], C.ps_g)[half]

            def qkm(kb, half):
                ks = slice(kb * 128, (kb + 1) * 128)
                ps = sbank(kb, half)
                o = ps.t[:, :].rearrange("p (a b) -> p a b", a=4)
                kc = slice((kb0 + kb) * 128, (kb0 + kb + 1) * 128)
                mm(o, C.kT.t[:, kc], qT.t[:, 4 * half:4 * half + 4, qs], True, False, [C.kT.bb[kb0 + kb], qT.b], [ps.b])
                mm(o, C.mball.t[:, ks], C.ident4.t[:, :, :], False, True, [C.mball.b, C.ident4.b], [ps.b])

            def expo(kb, half):
                pT = C.pT[2 * (kb % 2) + half]
                act(pT.t[:], sbank(kb, half).t[:], AF.Exp, [sbank(kb, half).b], [pT.b], scale=0.125)

            def pvm(kb, half):
                pT = C.pT[2 * (kb % 2) + half]
                pv = C.ps_pv[half]
                mm(pv.t[0:65, 0:512], C.Vc.t[:, kb0 + kb, half, :], pT.t[:, :], kb == 0, kb == nkb - 1,
                   [pT.b, C.Vc.bb[kb0 + kb]], [pv.b])
            oodd = C.t16.t[:, :].rearrange("p (a b) -> p a b", a=4)
            for half in range(2):
                qkm(0, half)
                expo(0, half)
            for kb in range(nkb):
                for half in range(2):
                    if kb + 1 < nkb:
                        qkm(kb + 1, half)
                    pvm(kb, half)
                    if kb + 1 < nkb:
                        expo(kb + 1, half)
                    yield
            for half in range(2):
                pv = C.ps_pv[half]
                rd = C.rl[half]
                ps = C.ps_s[half]
                S.op("dve", lambda e, pv=pv, rd=rd: e.reciprocal(out=rd.t[64:65, 0:512], in_=pv.t[64:65, 0:512]),
                     reads=[pv.b], writes=[rd.b])
                mm(ps.t[0:64, 0:512], C.onesf.t[64:65, 0:64], rd.t[64:65, 0:512], True, True, [C.onesf.b, rd.b], [ps.b])
                cp("act", rd.t[0:64, 0:512], ps.t[0:64, 0:512], [ps.b], [rd.b])
                pv4 = pv.t[0:64, 0:512].rearrange("p (a b) -> p a b", a=4)
                rd4 = rd.t[0:64, 0:512].rearrange("p (a b) -> p a b", a=4)
                tt("dve", yT.t[0:64, 4 + 2 * half:6 + 2 * half, qs], pv4[:, 0:4:2, :], rd4[:, 0:4:2, :], ALU.mult,
                   [pv.b, rd.b], [yT.b])
                tt("dve", oodd[0:64, 2 * half:2 * half + 2, :], pv4[:, 1:4:2, :], rd4[:, 1:4:2, :], ALU.mult,
                   [pv.b, rd.b], [C.t16.b])
            S.dma("sp", yT.t[64:128, 4:8, qs], oodd[0:64, :, :], reads=[C.t16.b], writes=[yT.b])
            yield

        def dense(ng, x_src, y_dst, yT):
            Wn = ng * 128
            xg = C.xg
            BK = (C.ps_mm[0], C.ps_s[0])
            PO = C.ps_s[1]
            S.dma("sp", xg.t[:, 0:ng, :], x_src.rearrange("(n p) d -> p n d", p=128), writes=[xg.b])
            cnt = 0
            for ch in range(2):
                w = stream(I.w_o[ch])
                for gi in range(ng):
                    PS = BK[cnt % 2]
                    cnt += 1
                    for k in range(8):
                        mm(PS.t[:, 0:512], yT.t[:, k, gi * 128:(gi + 1) * 128], w.t[:, k, :], k == 0, k == 7,
                           [w.b, yT.b], [PS.b])
                    xs = xg.t[:, gi, ch * 512:(ch + 1) * 512]
                    tt("dve", xs, PS.t[:, 0:512], xs, ALU.add, [PS.b, xg.b], [xg.b])
                    yield
            for gi in range(ng):
                norm_transpose(xg.t[:, gi, :], xg.b, C_GFFN, C.hTg, gi * 128)
                yield
            NU = 2

            def epilogue(PS, fl):
                fa, fb = C.ft[0], C.ft[1]
                act(fa.t[:, 0:Wn], PS.t[:, 0:Wn], AF.Tanh, [PS.b], [fa.b], scale=0.5)
                stt(fb.t[:, 0:Wn], fa.t[:, 0:Wn], 1.0, PS.t[:, 0:Wn], ALU.add, ALU.mult, [fa.b, PS.b], [fb.b])
                stt(C.actT.t[:, fl, 0:Wn], fb.t[:, 0:Wn], 0.5, PS.t[:, Wn:2 * Wn], ALU.mult, ALU.mult,
                    [fb.b, PS.b], [C.actT.b])
            for u0 in range(0, 11, NU):
                u1 = min(11, u0 + NU)
                f0 = 2 * u0
                nf = 2 * (u1 - u0)
                pend = None
                for u in range(u0, u1):
                    w = stream(I.w_fi[u])
                    for e_ in range(2):
                        fl = 2 * (u - u0) + e_
                        PS = BK[cnt % 2]
                        cnt += 1
                        for k in range(8):
                            mm(PS.t[:, 0:Wn], w.t[:, k, e_ * 256:e_ * 256 + 128], C.hTg.t[:, k, 0:Wn], k == 0, k == 7,
                               [w.b, C.hTg.b], [PS.b])
                        for k in range(8):
                            mm(PS.t[:, Wn:2 * Wn], w.t[:, k, e_ * 256 + 128:e_ * 256 + 256], C.hTg.t[:, k, 0:Wn], False, k == 7,
                               [w.b, C.hTg.b], [PS.b])
                        if pend is not None:
                            epilogue(*pend)
                        pend = (PS, fl)
                        yield
                epilogue(*pend)
                for ch in range(2):
                    w = stream(I.w_fo[ch][:, f0:f0 + nf, :], nk=nf)
                    for gi in range(ng):
                        for fl in range(nf):
                            mm(PO.t[:, 0:512], C.actT.t[:, fl, gi * 128:(gi + 1) * 128], w.t[:, fl, :],
                               fl == 0, fl == nf - 1, [w.b, C.actT.b], [PO.b])
                        xs = xg.t[:, gi, ch * 512:(ch + 1) * 512]
                        tt("dve", xs, PO.t[:, 0:512], xs, ALU.add, [PO.b, xg.b], [xg.b])
                        yield
            S.dma("sp", y_dst.rearrange("(n p) d -> p n d", p=128), xg.t[:, 0:ng, :], reads=[xg.b], writes=[])
            yield

        def store_state(dst):
            S.dma("sp", dst[0].rearrange("(c p) -> p c", p=128), C.hstate.t[:, :], reads=[C.hstate.b], writes=[],
                  allow_slow_non_contiguous=True)
            for r in range(3):
                S.dma("sp", dst[1 + r].rearrange("(c p) -> p c", p=128), C.xtail.t[:, :, r], reads=[C.xtail.b], writes=[],
                      allow_slow_non_contiguous=True)

        def drain(gen):
            for _ in gen:
                pass

        NSLOT = NB // 2
        items = []
        if with_sample:
            items.append(dict(kind="smp"))
        for slot in range(NSLOT):
            items.append(dict(kind="slot", slot=slot))

        def light_slot(slot):
            r0 = slot * 256
            return light(I.x_all[r0:r0 + 256, :], 2, 2 * slot, I.cos_all[r0:r0 + 256, :], I.sin_all[r0:r0 + 256, :], slot == 0,
                         O.kvk_all[r0:r0 + 256, :], 256)

        def prompt_reset():
            memset("pool", C.hstate.t[:], 0.0, [C.hstate.b])
            memset("pool", C.xtail.t[:], 0.0, [C.xtail.b])
            yield

        if with_sample:
            NPB = PAST // 128
            S.dma("sp", C.cm.t[:], I.cm_smp, writes=[C.cm.b])
            stg = C.wb[0]
            sv = stg.t[:].rearrange("p k (a c) -> p (k a) c", c=128)
            cks = I.cache_k.rearrange("(b p) c -> p b c", p=128)
            for r0 in range(0, NPB, 8):
                S.dma("pool", sv[:, r0:r0 + 8, :], cks[:, r0:r0 + 8, :], writes=[stg.b])
            for r0 in range(0, NPB, 8):
                for j in range(8):
                    tr(C.ps_tr.t[:, j, :], sv[:, r0 + j, :], [stg.b], [C.ps_tr.b])
                cp("act", C.kT.t[:, (SMP_OFF + r0) * 128:(SMP_OFF + r0 + 8) * 128], C.ps_tr.t[:].rearrange("p a b -> p (a b)"), [C.ps_tr.b],
                   C.kT.bb[SMP_OFF + r0:SMP_OFF + r0 + 8])
            stg2 = C.wb[1]
            sv2 = stg2.t[:].rearrange("p k (a c) -> p (k a) c", c=128)
            kis = I.cache_ki.rearrange("(b p) c -> p b c", p=128)
            for r0 in range(0, NPB, 8):
                S.dma("pool", sv2[:, r0:r0 + 8, 0:64], kis[:, r0:r0 + 8, :], writes=[stg2.b])
                S.dma("pool", sv2[:, r0:r0 + 8, 64:128], kis[:, r0:r0 + 8, :], writes=[stg2.b])
            for r0 in range(0, NPB, 8):
                for j in range(8):
                    tr(C.ps_tr.t[:, j, :], sv2[:, r0 + j, :], [stg2.b], [C.ps_tr.b])
                cp("act", C.kiT.t[:, (SMP_OFF + r0) * 128:(SMP_OFF + r0 + 8) * 128], C.ps_tr.t[:].rearrange("p a b -> p (a b)"), [C.ps_tr.b],
                   C.kiT.bb[SMP_OFF + r0:SMP_OFF + r0 + 8])
            cvs = I.cache_v.rearrange("(b p) (h d) -> p b h d", p=128, h=2)
            for r0 in range(0, NPB, 8):
                for hh in range(2):
                    S.dma("pool", C.Vc.t[:, SMP_OFF + r0:SMP_OFF + r0 + 8, hh, 0:64], cvs[:, r0:r0 + 8, hh, :],
                          writes=C.Vc.bb[SMP_OFF + r0:SMP_OFF + r0 + 8])
            S.dma("sp", C.hstate.t[:, :], I.state_h.rearrange("(c p) -> p c", p=128), writes=[C.hstate.b],
                  allow_slow_non_contiguous=True)
            for r in range(3):
                S.dma("sp", C.xtail.t[:, :, r], I.state_conv[r].rearrange("(c p) -> p c", p=128), writes=[C.xtail.b],
                      allow_slow_non_contiguous=True)
            drain(light(I.x_smp, 1, SMP_OFF + NPB, I.cos_smp, I.sin_smp, False, O.kvk_smp, DEC))
            store_state(O.hc_s)
        else:
            drain(prompt_reset())
            drain(light_slot(0))

        prevY = None
        cur_dense = [None, 0]
        own_pre = None
        pend_dense = None
        for idx, it in enumerate(items):
            if it["kind"] == "smp":
                par, gi, nkb, ng, kb0 = 1, 0, PAST // 128 + 1, 1, SMP_OFF
                own_proj(I.x_smp, 1, I.cos_smp, I.sin_smp, C.qTg[1])
                make_ya(0, True, C.yTg[1])
                new_group = True
            else:
                slot = it["slot"]
                grp, gi = slot // G, slot % G
                par = grp % 2
                kb0 = 0
                nkb = 2 * slot + 2
                new_group = gi == 0
                late_own = None
                if new_group:
                    xo, co, so = (I.x_own[grp * W:(grp + 1) * W, :], I.cos_own[grp * W:(grp + 1) * W, :],
                                  I.sin_own[grp * W:(grp + 1) * W, :])
                    own_proj(xo, G, co, so, C.qTg[par], pre=own_pre, parts=("norm", "wi", "qi"))

                    def late_own(xo=xo, co=co, so=so, par=par, pre=own_pre):
                        own_proj(xo, G, co, so, C.qTg[par], pre=pre, parts=("gate", "q"), bank0=True)
                        yield
                    own_pre = None
                if slot == 0:
                    S.dma("sp", C.cm.t[:], I.cm_own, writes=[C.cm.b])
                if late_own is None:
                    make_ya(gi, False, C.yTg[par])

            def step(gen):
                try:
                    next(gen)
                    return True
                except StopIteration:
                    return False
            def step(gen):
                try:
                    next(gen)
                    return True
                except StopIteration:
                    return False
            if pend_dense is not None and not new_group:
                cur_dense[0] = pend_dense()
                cur_dense[1] = 54
                pend_dense = None
            others = []
            if idx + 1 < len(items):
                if it["kind"] == "smp":
                    others.append(prompt_reset())
                others.append(light_slot(items[idx + 1]["slot"]))
            dense_budget = 0
            if cur_dense[0] is not None:
                dense_budget = 10 ** 6
            def record(gens):
                rec = []
                real_op, real_dma = S.op, S.dma
                S.op = lambda *a_, **k_: rec.append((real_op, a_, k_))
                S.dma = lambda *a_, **k_: rec.append((real_dma, a_, k_))
                try:
                    for g_ in gens:
                        for _ in g_:
                            pass
                finally:
                    S.op, S.dma = real_op, real_dma
                return rec
            micro = []
            if it["kind"] == "slot" and late_own is not None:
                micro = record([late_own()])
            dgen = cur_dense[0]
            cur_dense[0] = None
            idx_gen = indexer(gi, nkb, kb0)
            n_idx = 8 * ((nkb + 3) // 4)
            n_light_heads = n_idx if dgen is None else max(8, n_idx // 4)
            per = len(micro) / float(n_light_heads)
            perd = (54.0 / max(1, n_idx - n_light_heads)) if dgen is not None else 0.0
            acc = 0.0
            accd = 0.0
            pos = 0
            while step(idx_gen):
                if pos < len(micro):
                    acc += per
                    while acc >= 1.0 and pos < len(micro):
                        acc -= 1.0
                        f_, a_, k_ = micro[pos]
                        pos += 1
                        f_(*a_, **k_)
                elif dgen is not None:
                    accd += perd
                    while dgen is not None and accd >= 1.0:
                        accd -= 1.0
                        if not step(dgen):
                            dgen = None
            while pos < len(micro):
                f_, a_, k_ = micro[pos]
                pos += 1
                f_(*a_, **k_)
            if dgen is not None:
                drain(dgen)
            if it["kind"] == "slot" and late_own is not None:
                make_ya(gi, False, C.yTg[par])
            if idx + 1 < len(items) and items[idx + 1]["slot"] % G == 0:
                g2 = items[idx + 1]["slot"] // G
                own_pre = own_prefetch(I.x_own[g2 * W:(g2 + 1) * W, :], G)
            att_gen, n_att = (prevY if prevY is not None else (None, 0))
            micro2 = record(list(others))
            bis_gen = bisect(nkb, None)
            acc = 0.0
            accl = 0.0
            posl = 0
            while step(bis_gen):
                acc += n_att / float(NIT + 1)
                accl += len(micro2) / float(NIT)
                while att_gen is not None and acc >= 1.0:
                    acc -= 1.0
                    if not step(att_gen):
                        att_gen = None
                while accl >= 1.0 and posl < len(micro2):
                    accl -= 1.0
                    f_, a_, k_ = micro2[posl]
                    posl += 1
                    f_(*a_, **k_)
            while att_gen is not None and step(att_gen):
                pass
            while posl < len(micro2):
                f_, a_, k_ = micro2[posl]
                posl += 1
                f_(*a_, **k_)
            prevY = None
            maskgen(nkb)
            prevY = (attention(gi, nkb, C.yTg[par], C.qTg[par], kb0), 2 * nkb + 1)
            if it["kind"] == "smp":
                pend_dense = (lambda: dense(1, I.x_smp, O.y_smp, C.yTg[1]))
            elif gi == G - 1:
                pend_dense = (lambda grp=grp, par=par: dense(G, I.x_own[grp * W:(grp + 1) * W, :],
                                                             O.y_own[grp * W:(grp + 1) * W, :], C.yTg[par]))
        drain(prevY[0])
        if cur_dense[0] is not None:
            drain(cur_dense[0])
        drain(pend_dense())
        store_state(O.hc_p)
        S.finish()
        S.emit()
        print("instructions:", S.nins, {e: len(S.prog[e]) for e in S.prog})
    return nc


def _rope_tables(pos):
    half = 32
    inv = (10000.0 ** (-np.arange(half, dtype=np.float32) / half)).astype(np.float32)
    ang = pos.astype(np.float32)[:, None] * inv[None, :]
    return np.cos(ang).astype(np.float32), np.sin(ang).astype(np.float32)


def _prep_shared(w_in, w_out, w_ffn_in, w_ffn_out, conv_w, conv_b, b_rg, b_ig, lru_lambda, norm_mix, norm_ffn):
    sh = {}
    xr, gate, q, k, v, qi, ki, wi = np.split(w_in, np.cumsum([512, 512, 512, 128, 128, 512, 64])[:], axis=1)
    sh["w_light"] = np.ascontiguousarray(np.concatenate([xr, k, v, ki], axis=1))
    qp = np.concatenate([np.concatenate([q[:, j * 64:(j + 1) * 64], q[:, (j + 4) * 64:(j + 5) * 64]], axis=1) for j in range(4)], axis=1)

    def pk(w):
        return w.reshape(8, 128, 512).transpose(1, 0, 2)
    sh["w_own"] = np.ascontiguousarray(np.stack([pk(gate), pk(qp), pk(qi)], axis=0))
    sh["w_wi"] = np.ascontiguousarray(wi)
    sh["w_o"] = np.ascontiguousarray(np.stack([pk(w_out[:, 0:512]), pk(w_out[:, 512:1024])], axis=0))
    g_, u_ = w_ffn_in[:, :DFF], w_ffn_in[:, DFF:]
    units = []
    for u in range(11):
        cols = np.concatenate([g_[:, (2 * u) * 128:(2 * u + 1) * 128], u_[:, (2 * u) * 128:(2 * u + 1) * 128],
                               g_[:, (2 * u + 1) * 128:(2 * u + 2) * 128], u_[:, (2 * u + 1) * 128:(2 * u + 2) * 128]], axis=1)
        units.append(pk(cols))
    sh["w_fi"] = np.ascontiguousarray(np.stack(units, axis=0))
    sh["w_fo"] = np.ascontiguousarray(np.stack([w_ffn_out[:, ch * 512:(ch + 1) * 512].reshape(22, 128, 512).transpose(1, 0, 2)
                                                for ch in range(2)], axis=0))
    colsrc = np.zeros((128, NCOLP), np.float32)

    def colset(c0, vec, n):
        colsrc[:, c0:c0 + n] = vec.reshape(n, 128).T
    colset(C_GMIX, norm_mix, 8)
    colset(C_GFFN, norm_ffn, 8)
    for j in range(4):
        colset(C_CW + 4 * j, conv_w[j], 4)
    colset(C_CB, conv_b, 4)
    colset(C_HBRG, b_rg.reshape(-1), 4)
    colset(C_HBIG, b_ig.reshape(-1), 4)
    colset(C_LAM, lru_lambda, 4)
    colsrc[:, C_NEG] = NEG
    sh["colsrc_base"] = colsrc
    return sh


def _run(inputs, NB, with_sample=True):
    f = lambda a: np.ascontiguousarray(np.asarray(a, dtype=np.float32))
    x_prompt = f(inputs["x_prompt"])[:, :NB * 128]
    x_sample = f(inputs["x_sample"])
    sh = _prep_shared(f(inputs["w_in"]), f(inputs["w_out"]), f(inputs["w_ffn_in"]), f(inputs["w_ffn_out"]),
                      f(inputs["conv_w"]), f(inputs["conv_b"]), f(inputs["b_rg"]), f(inputs["b_ig"]),
                      f(inputs["lru_lambda"]), f(inputs["norm_mix"]), f(inputs["norm_ffn"]))
    cos_all, sin_all = _rope_tables(np.arange(NB * 128))
    cs_, ss_ = _rope_tables(PAST + np.arange(DEC))
    cos_smp = np.zeros((128, 32), np.float32); cos_smp[:DEC] = cs_
    sin_smp = np.zeros((128, 32), np.float32); sin_smp[:DEC] = ss_
    pw = np.tile((2.0 ** -(np.arange(NIT) + 1.0)).astype(np.float32)[None, :], (128, 1))
    BIG = np.float32(-1e30)
    t = np.arange(128)[:, None]
    s = np.arange(256)[None, :]
    cm = []
    for par in range(2):
        qpos = par * 128 + t
        cm.append(np.where((qpos // 64) >= (s // 64), np.float32(0), BIG).astype(np.float32))
    cm_smp = np.where(s < 128 + DEC, np.float32(0), BIG).astype(np.float32) * np.ones((128, 1), np.float32)
    NOWN = NB // 2
    in_maps = []
    for c in range(8):
        b, par = c // 2, c % 2
        xa = x_prompt[b]
        own_idx = (np.arange(NOWN)[:, None] * 256 + par * 128 + np.arange(128)[None, :]).reshape(-1)
        colsrc = sh["colsrc_base"].copy()
        colsrc[:, C_PAR] = float(par)
        xs = np.zeros((128, D), np.float32); xs[:DEC] = x_sample[c]
        m = {
            "x_all": xa, "x_own": np.ascontiguousarray(xa[own_idx]), "x_smp": xs,
            "cache_k": f(inputs["cache_k"])[c].reshape(PAST, 128), "cache_v": f(inputs["cache_v"])[c].reshape(PAST, 128),
            "cache_ki": f(inputs["cache_kidx"])[c], "state_h": f(inputs["state_h"])[c], "state_conv": f(inputs["state_conv"])[c],
            "w_light": sh["w_light"], "w_own": sh["w_own"], "w_wi": sh["w_wi"], "w_o": sh["w_o"], "w_fi": sh["w_fi"], "w_fo": sh["w_fo"],
            "w_rg": f(inputs["w_rg"]), "w_ig": f(inputs["w_ig"]), "colsrc": colsrc,
            "gq": f(inputs["q_norm"]), "gk": f(inputs["k_norm"]), "ident": np.eye(128, dtype=np.float32), "pw": pw,
            "cos_all": cos_all, "sin_all": sin_all, "cos_own": np.ascontiguousarray(cos_all[own_idx]),
            "sin_own": np.ascontiguousarray(sin_all[own_idx]), "cos_smp": cos_smp, "sin_smp": sin_smp,
            "cm_own": cm[par], "cm_smp": cm_smp,
        }
        in_maps.append(m)
    nc = build(NB, with_sample)
    res = run_bass_kernel_spmd(nc, in_maps, core_ids=list(range(8)))
    R = res.results
    Tn = NB * 128
    y_p = np.zeros((4, Tn, D), np.float32)
    k_p = np.zeros((4, Tn, 2, 64), np.float32); v_p = np.zeros((4, Tn, 2, 64), np.float32); ki_p = np.zeros((4, Tn, 64), np.float32)
    h_p = np.zeros((4, LW), np.float32); conv_p = np.zeros((4, 3, LW), np.float32)
    y_s = np.zeros((8, DEC, D), np.float32)
    k_s = np.zeros((8, DEC, 2, 64), np.float32); v_s = np.zeros((8, DEC, 2, 64), np.float32); ki_s = np.zeros((8, DEC, 64), np.float32)
    h_s = np.zeros((8, LW), np.float32); conv_s = np.zeros((8, 3, LW), np.float32)
    for c in range(8):
        b, par = c // 2, c % 2
        r = R[c]
        y_p[b].reshape(NOWN, 2, 128, D)[:, par] = r["y_own"].reshape(NOWN, 128, D)
        if par == 0:
            kvk = r["kvk_all"]
            k_p[b] = kvk[:, 0:128].reshape(Tn, 2, 64); v_p[b] = kvk[:, 128:256].reshape(Tn, 2, 64); ki_p[b] = kvk[:, 256:320]
            h_p[b] = r["hc_p"][0]; conv_p[b] = r["hc_p"][1:4]
        if with_sample:
            y_s[c] = r["y_smp"][:DEC]
            kvk = r["kvk_smp"][:DEC]
            k_s[c] = kvk[:, 0:128].reshape(DEC, 2, 64); v_s[c] = kvk[:, 128:256].reshape(DEC, 2, 64); ki_s[c] = kvk[:, 256:320]
            h_s[c] = r["hc_s"][0]; conv_s[c] = r["hc_s"][1:4]
    return (y_p, y_s, k_p, v_p, ki_p, h_p, conv_p, k_s, v_s, ki_s, h_s, conv_s)


def kernel(**inputs):
    return _run(inputs, SEQ // 128, True)
```

```python
import numpy as np
from contextlib import ExitStack
import concourse.bass as bass
import concourse.mybir as mybir
from concourse.bass_utils import run_bass_kernel_spmd

F32 = mybir.dt.float32
BF16 = mybir.dt.bfloat16
U8 = mybir.dt.uint8
FP8 = mybir.dt.float8e5
AF = mybir.ActivationFunctionType
ALU = mybir.AluOpType
AX = mybir.AxisListType

D = 1024
SEQ = 8192
PAST = 4096
DEC = 16
LW = 512
DFF = 2816
TOPK = 256
EPS = 1e-6
G = 2
BIS_DVE = 0.66
NIT = 20
NEG = -30000.0


class Buf:
    __slots__ = ("name", "w", "r")

    def __init__(self, name):
        self.name = name
        self.w = {}
        self.r = {}


class TL:
    def __init__(self, t, name, nb=0):
        self.t = t
        self.b = Buf(name)
        self.bb = [Buf("%s%d" % (name, i)) for i in range(nb)]


CENG = ("pe", "act", "dve", "pool")


class Sched:
    NDMA = 6

    def __init__(self, nc, stack):
        self.nc = nc
        self.streams = ("pe", "act", "dve", "pool", "sp")
        self.prog = {e: [] for e in self.streams}
        self.sems = {}
        self.ops = {e: [] for e in CENG}
        for e in CENG:
            self.sems[e] = stack.enter_context(nc.semaphore("s_" + e))
        self.dq = {}
        for q in ("sp", "pool"):
            for i in range(self.NDMA):
                self.sems["d_%s%d" % (q, i)] = stack.enter_context(nc.semaphore("d_%s%d" % (q, i)))
            self.dq[q] = 0
        self.waited = {e: {} for e in self.streams}
        self.nins = 0

    def _deps(self, engid, reads, writes):
        need = {}
        is_dma = engid.startswith("q_")

        def add(k, v, e, same_ok):
            if e == engid and not same_ok and engid == "pe":
                return
            if need.get(k, 0) < v:
                need[k] = v
        for b in reads:
            for k, (v, e) in b.w.items():
                add(k, v, e, True)
        for b in writes:
            for k, (v, e) in b.w.items():
                if is_dma and e.startswith("q_"):
                    continue
                add(k, v, e, False)
            for k, (v, e) in b.r.items():
                add(k, v, e, False)
        return need

    def _emit_waits(self, stream, need):
        for k, v in need.items():
            if self.waited[stream].get(k, 0) >= v:
                continue
            self.waited[stream][k] = v
            if k in self.ops:
                self.ops[k][v - 1][1] = True
            self.prog[stream].append(("wait", k, v))

    def _update(self, tok, reads, writes):
        k, v, e = tok
        for b in reads:
            if b.r.get(k, (0, None))[0] < v:
                b.r[k] = (v, e)
        for b in writes:
            if e.startswith("q_") and b.w and all(x[1].startswith("q_") for x in b.w.values()):
                b.w[k] = (v, e)
            else:
                b.w = {k: (v, e)}
            b.r = {}

    def op(self, eng, fn, reads=(), writes=()):
        need = self._deps(eng, reads, writes)
        self._emit_waits(eng, need)
        rec = [fn, False]
        self.ops[eng].append(rec)
        v = len(self.ops[eng])
        self.prog[eng].append(("op", rec))
        self._update((eng, v, eng), reads, writes)
        self.nins += 1

    def dma(self, q, out, in_, reads=(), writes=(), **kw):
        i = self.dq[q]
        self.dq[q] += 1
        slot = i % self.NDMA
        gen = i // self.NDMA + 1
        k = "d_%s%d" % (q, slot)
        need = self._deps("q_" + q, reads, writes)
        if gen > 1:
            need[k] = max(need.get(k, 0), 16 * (gen - 1))
        self._emit_waits(q, need)
        self.prog[q].append(("dma", k, out, in_, kw))
        self._update((k, 16 * gen, "q_" + q), reads, writes)
        self.nins += 1

    def finish(self):
        need = {}
        for q in ("sp", "pool"):
            n = self.dq[q]
            for slot in range(self.NDMA):
                c = (n - slot + self.NDMA - 1) // self.NDMA
                if c > 0:
                    need["d_%s%d" % (q, slot)] = 16 * c
        for e in CENG:
            if self.ops[e]:
                need[e] = len(self.ops[e])
        self._emit_waits("sp", need)

    def emit(self):
        nc = self.nc
        cum = {}
        for e in CENG:
            c = 0
            arr = []
            for rec in self.ops[e]:
                if rec[1]:
                    c += 1
                arr.append(c)
            cum[e] = arr
        sems = self.sems

        def run(stream, e):
            for it in self.prog[stream]:
                if it[0] == "wait":
                    _, k, v = it
                    val = cum[k][v - 1] if k in cum else v
                    e.wait_ge(sems[k], val)
                elif it[0] == "op":
                    rec = it[1]
                    ins = rec[0](e)
                    if rec[1]:
                        ins.then_inc(sems[stream], 1)
                else:
                    _, k, out, in_, kw = it
                    e.dma_start(out=out, in_=in_, **kw).then_inc(sems[k], 16)
        with nc.Block() as block:
            @block.sync
            def _(e):
                run("sp", e)

            @block.tensor
            def _(e):
                run("pe", e)

            @block.scalar
            def _(e):
                run("act", e)

            @block.vector
            def _(e):
                run("dve", e)

            @block.gpsimd
            def _(e):
                run("pool", e)


C_GMIX, C_GFFN, C_CW, C_CB, C_HBRG, C_HBIG, C_HNSP, C_PAR, C_NEG, C_LAM, C_TMP, C_MH = 0, 8, 16, 32, 36, 40, 44, 48, 49, 50, 54, 58
NCOLP = 64


class Ctx:
    pass


def interleave(fg, bgs, ratio=1):
    bgs = list(bgs)

    def bg_step():
        while bgs:
            try:
                next(bgs[0])
                return True
            except StopIteration:
                bgs.pop(0)
        return False
    for _ in fg:
        for _r in range(ratio):
            if not bg_step():
                break
    while bg_step():
        pass


def build(NB, with_sample=True):
    NOWN = NB // 2
    NGRP = NOWN // G
    NKB = max(NB, min(NB + 33, 64))
    SMP_OFF = NB if NB + 33 <= NKB else NKB - 33
    W = G * 128
    nc = bass.Bass("TRN2", target_bir_lowering=False)

    def din(name, shape):
        return nc.dram_tensor(name, list(shape), F32, kind="ExternalInput").ap()

    def dout(name, shape):
        return nc.dram_tensor(name, list(shape), F32, kind="ExternalOutput").ap()

    I = Ctx()
    I.x_all = din("x_all", [NB * 128, D])
    I.x_own = din("x_own", [NOWN * 128, D])
    I.x_smp = din("x_smp", [128, D])
    I.cache_k = din("cache_k", [PAST, 128])
    I.cache_v = din("cache_v", [PAST, 128])
    I.cache_ki = din("cache_ki", [PAST, 64])
    I.state_h = din("state_h", [LW])
    I.state_conv = din("state_conv", [3, LW])
    I.w_light = din("w_light", [D, 832])
    I.w_own = din("w_own", [3, 128, 8, 512])
    I.w_wi = din("w_wi", [D, 8])
    I.w_o = din("w_o", [2, 128, 8, 512])
    I.w_fi = din("w_fi", [11, 128, 8, 512])
    I.w_fo = din("w_fo", [2, 128, 22, 512])
    I.w_rg = din("w_rg", [8, 64, 64])
    I.w_ig = din("w_ig", [8, 64, 64])
    I.colsrc = din("colsrc", [128, NCOLP])
    I.gq = din("gq", [64])
    I.gk = din("gk", [64])
    I.ident = din("ident", [128, 128])
    I.pw = din("pw", [128, NIT])
    I.cos_all = din("cos_all", [NB * 128, 32])
    I.sin_all = din("sin_all", [NB * 128, 32])
    I.cos_own = din("cos_own", [NOWN * 128, 32])
    I.sin_own = din("sin_own", [NOWN * 128, 32])
    I.cos_smp = din("cos_smp", [128, 32])
    I.sin_smp = din("sin_smp", [128, 32])
    I.cm_own = din("cm_own", [128, 256])
    I.cm_smp = din("cm_smp", [128, 256])
    O = Ctx()
    O.y_own = dout("y_own", [NOWN * 128, D])
    O.kvk_all = dout("kvk_all", [NB * 128, 320])
    O.hc_p = dout("hc_p", [4, LW])
    O.y_smp = dout("y_smp", [128, D])
    O.kvk_smp = dout("kvk_smp", [128, 320])
    O.hc_s = dout("hc_s", [4, LW])

    with ExitStack() as st:
        S = Sched(nc, st)

        def T(name, shape, dt, nb=0):
            return TL(st.enter_context(nc.sbuf_tensor("sb_" + name, list(shape), dt)), name, nb)

        def P(name, shape, dt):
            return TL(st.enter_context(nc.psum_tensor(name, list(shape), dt)), name)

        C = Ctx()
        C.ident4 = T("ident4", [128, 4, 128], BF16)
        C.wl = T("wl", [128, 8, 832], BF16)
        C.wwi = T("wwi", [128, 8, 8], BF16)
        C.wrg = T("wrg", [128, 4, 128], BF16)
        C.wig = T("wig", [128, 4, 128], BF16)
        C.colp = T("colp", [128, NCOLP], F32)
        C.gq = T("gq", [128, 64], F32)
        C.gk = T("gk", [128, 64], F32)
        C.pw = T("pw", [128, NIT], F32)
        C.cm = T("cm", [128, 256], F32)
        C.kT = T("kT", [128, NKB * 128], BF16, NKB)
        C.kiT = T("kiT", [128, NKB * 128], BF16, NKB)
        C.Vc = T("Vc", [128, NKB, 2, 65], BF16, NKB)
        C.xt = T("xt", [128, D], F32)
        C.hb = T("hb", [128, D], BF16)
        C.hT = T("hT", [128, 8, 256], BF16)
        C.jk = T("jk", [128, 16], BF16)
        C.arena = T("arena", [128, NKB * 128], F32, (NKB + 3) // 4)
        C.L_xr = T("L_xr", [128, 4, 259], F32)
        C.L_xc = T("L_xc", [128, 4, 256], F32)
        C.L_r = T("L_r", [128, 4, 256], F32)
        C.L_i = T("L_i", [128, 4, 256], F32)
        C.xcb = T("xcb", [128, 4, 256], BF16)
        C.xtail = T("xtail", [128, 4, 3], F32)
        C.hstate = T("hstate", [128, 4], F32)
        C.kv = T("kv", [128, 320], F32)
        C.kout = T("kout", [128, 320], F32)
        C.rp = [T("rp%d" % i, [128, 8, 32], F32) for i in range(3)]
        C.kb16 = T("kb16", [128, 256], BF16)
        C.cs = T("cs", [128, 2, 32], F32)
        C.stats = [T("stat%d" % i, [128, 24], F32) for i in range(8)]
        C.stat_i = 0
        C.mhalf = T("mhalf", [128, 8], F32)
        C.xg = T("xg", [128, G, D], F32)
        C.hTg = T("hTg", [128, 8, W], BF16)
        C.sgT = T("sgT", [128, 4, W], F32)
        C.qTg = [T("qTg%d" % i, [128, 8, W], BF16) for i in range(2)]
        C.qiTg = T("qiTg", [128, 4, W], BF16)
        C.w8g = T("w8g", [128, G, 8], F32)
        C.yTg = [T("yTg%d" % i, [128, 8, W], BF16) for i in range(2)]
        C.t16 = T("t16", [128, 512], BF16)
        C.cso = T("cso", [128, 2, 32], F32)
        C.rl = [T("rl%d" % i, [128, 512], F32) for i in range(2)]
        C.mball = T("mball", [128, NKB * 128], FP8)
        C.pT = [T("pT%d" % i, [128, 512], BF16) for i in range(4)]
        C.bis = T("bis", [128, 12 + NIT], F32)
        C.bisA = T("bisA", [128, 2], F32)
        C.bisP = T("bisP", [128, 2], F32)
        C.bmid = T("bmid", [128, 2], F32)
        C.bcnt = T("bcnt", [128, 2], F32)
        C.bg = T("bg", [128, 2], F32)
        C.onesf = T("onesf", [128, 64], F32)
        C.rc = T("rc", [128, 8], F32)
        C.actT = T("actT", [128, 4, W], BF16)
        C.ft = [T("ft%d" % i, [128, 512], F32) for i in range(2)]
        C.wb = [T("wb%d" % i, [128, 8, 512], BF16) for i in range(2)]
        C.wb_i = 0
        C.ps_mm = [P("ps_mm%d" % i, [128, 512], F32) for i in range(2)]
        C.ps_tr = P("ps_tr", [128, 8, 128], BF16)
        C.ps_s = [P("ps_s%d" % i, [128, 512], F32) for i in range(2)]
        C.ps_pv = [P("ps_pv%d" % i, [128, 512], F32) for i in range(2)]
        C.pvv = [p_.t[:, 0:260].rearrange("p (h d) -> p h d", h=4) for p_ in C.ps_pv]
        C.ps_g = P("ps_g", [128, 512], F32)
        ident = C.ident4.t[:, 0, :]
        IDB = C.ident4.b

        def stat():
            s = C.stats[C.stat_i % 8]
            C.stat_i += 1
            return s

        def next_wb():
            w = C.wb[C.wb_i % 2]
            C.wb_i += 1
            return w

        colp = C.colp.t

        def col(i):
            return colp[:, i:i + 1]

        def mm(out, lhsT, rhs, start, stop, reads, writes):
            S.op("pe", lambda e: e.matmul(out=out, lhsT=lhsT, rhs=rhs, start=start, stop=stop,
                                          skip_group_check=True), reads=reads, writes=writes)

        def tr(out, in_, reads, writes):
            S.op("pe", lambda e: e.transpose(out=out, in_=in_, identity=ident), reads=list(reads) + [IDB], writes=writes)

        def act(out, in_, func, reads, writes, **kw):
            S.op("act", lambda e: e.activation(out=out, in_=in_, func=func, **kw), reads=reads, writes=writes)

        def tt(eng, out, in0, in1, op, reads, writes):
            S.op(eng, lambda e: e.tensor_tensor(out=out, in0=in0, in1=in1, op=op), reads=reads, writes=writes)

        def ts(eng, out, in0, s1, s2, op0, op1, reads, writes, **kw):
            if op1 is None:
                S.op(eng, lambda e: e.tensor_scalar(out=out, in0=in0, scalar1=s1, scalar2=None, op0=op0, **kw),
                     reads=reads, writes=writes)
            else:
                S.op(eng, lambda e: e.tensor_scalar(out=out, in0=in0, scalar1=s1, scalar2=s2, op0=op0, op1=op1, **kw),
                     reads=reads, writes=writes)

        def stt(out, in0, scalar, in1, op0, op1, reads, writes):
            S.op("dve", lambda e: e.scalar_tensor_tensor(out=out, in0=in0, scalar=scalar, in1=in1, op0=op0, op1=op1),
                 reads=reads, writes=writes)

        def cp(eng, out, in_, reads, writes):
            if eng == "act":
                act(out, in_, AF.Identity, reads, writes)
            else:
                S.op(eng, lambda e: e.tensor_copy(out=out, in_=in_), reads=reads, writes=writes)

        def memset(eng, ap, val, writes):
            S.op(eng, lambda e: e.memset(ap, val), writes=writes)

        for r in range(4):
            S.dma("pool", C.ident4.t[:, r, :], I.ident, writes=[C.ident4.b])
        S.dma("pool", C.wl.t[:], I.w_light.rearrange("(k p) c -> p k c", p=128), writes=[C.wl.b])
        S.dma("pool", C.wwi.t[:], I.w_wi.rearrange("(k p) c -> p k c", p=128), writes=[C.wwi.b])
        memset("pool", C.wrg.t[:], 0.0, [C.wrg.b])
        memset("pool", C.wig.t[:], 0.0, [C.wig.b])
        for n in range(8):
            q, hf = n // 2, n % 2
            S.dma("pool", C.wrg.t[64 * hf:64 * hf + 64, q, 64 * hf:64 * hf + 64], I.w_rg[n], reads=[], writes=[C.wrg.b])
            S.dma("pool", C.wig.t[64 * hf:64 * hf + 64, q, 64 * hf:64 * hf + 64], I.w_ig[n], reads=[], writes=[C.wig.b])
        S.dma("sp", C.colp.t[:], I.colsrc, writes=[C.colp.b])
        S.dma("sp", C.gq.t[:], I.gq.partition_broadcast(128), writes=[C.gq.b])
        S.dma("sp", C.gk.t[:], I.gk.partition_broadcast(128), writes=[C.gk.b])
        S.dma("sp", C.pw.t[:], I.pw, writes=[C.pw.b])
        memset("pool", C.mhalf.t[:], -0.5, [C.mhalf.b])
        memset("pool", C.onesf.t[:], 1.0, [C.onesf.b])
        memset("pool", C.qTg[0].t[:], 0.0, [C.qTg[0].b])
        memset("pool", C.qTg[1].t[:], 0.0, [C.qTg[1].b])
        memset("pool", C.Vc.t[:], 1.0, [C.Vc.b] + C.Vc.bb)
        act(colp[:, C_TMP:C_TMP + 4], colp[:, C_LAM:C_LAM + 4], AF.Exp, [C.colp.b], [C.colp.b], scale=-1.0)
        act(colp[:, C_TMP:C_TMP + 4], colp[:, C_TMP:C_TMP + 4], AF.Ln, [C.colp.b], [C.colp.b], bias=1.0)
        ts("dve", colp[:, C_HNSP:C_HNSP + 4], colp[:, C_TMP:C_TMP + 4], -4.0, None, ALU.mult, None, [C.colp.b], [C.colp.b])
        ts("dve", colp[:, C_HBRG:C_HBRG + 8], colp[:, C_HBRG:C_HBRG + 8], 0.5, None, ALU.mult, None, [C.colp.b], [C.colp.b])

        def rstd_cols(s, n, inv_n):
            ts("dve", s.t[:, 16:16 + n], s.t[:, 0:n], inv_n, EPS, ALU.mult, ALU.add, [s.b], [s.b])
            tt("pool", s.t[:, 8:8 + n], s.t[:, 16:16 + n], C.mhalf.t[:, 0:n], ALU.pow, [s.b, C.mhalf.b], [s.b])

        def norm_transpose(src, src_b, gcol, dstT, tokoff):
            s = stat()
            act(C.jk.t[:, 0:1].to_broadcast([128, D]), src, AF.Square, [src_b], [s.b], accum_out=s.t[:, 0:1])
            rstd_cols(s, 1, 1.0 / D)
            act(C.hb.t[:], src, AF.Identity, [src_b, s.b], [C.hb.b], scale=s.t[:, 8:9])
            for j in range(8):
                tr(C.ps_tr.t[:, j, :], C.hb.t[:, j * 128:(j + 1) * 128], [C.hb.b], [C.ps_tr.b])
            for j in range(8):
                act(dstT.t[:, j, tokoff:tokoff + 128], C.ps_tr.t[:, j, :], AF.Identity,
                    [C.ps_tr.b, C.colp.b], [dstT.b], scale=col(gcol + j))

        def rope(eng, src, dst, H, cos, sin, reads, writes):
            cb = cos.unsqueeze(1).to_broadcast([128, H, 32])
            sb = sin.unsqueeze(1).to_broadcast([128, H, 32])
            x1, x2 = src[:, :, 0:32], src[:, :, 32:64]
            t = [C.rp[i].t[:, 0:H, :] for i in range(3)]
            tb = [C.rp[i].b for i in range(3)]
            tt(eng, t[0], x1, cb, ALU.mult, reads, [tb[0]])
            tt(eng, t[1], x2, sb, ALU.mult, reads, [tb[1]])
            tt(eng, t[2], x1, sb, ALU.mult, reads, [tb[2]])
            tt(eng, dst[:, :, 0:32], t[0], t[1], ALU.subtract, [tb[0], tb[1]] + reads, writes)
            tt(eng, t[0], x2, cb, ALU.mult, reads, [tb[0]])
            tt(eng, dst[:, :, 32:64], t[0], t[2], ALU.add, [tb[0], tb[2]] + reads, writes)

        def head_rmsnorm(eng, src, H, gain, src_b):
            s = stat()
            sq = C.ft[1]
            flat = src.rearrange("p h d -> p (h d)")
            act(sq.t[:, 0:H * 64], flat, AF.Square, [src_b], [sq.b])
            S.op("dve", lambda e: e.tensor_reduce(out=s.t[:, 0:H], in_=sq.t[:, 0:H * 64].rearrange("p (h d) -> p h d", h=H),
                                                  axis=AX.X, op=ALU.add), reads=[sq.b], writes=[s.b])
            rstd_cols(s, H, 1.0 / 64)
            tt(eng, src, src, s.t[:, 8:8 + H].unsqueeze(2).to_broadcast([128, H, 64]), ALU.mult, [src_b, s.b], [src_b])
            tt(eng, src, src, gain.t[:, :].unsqueeze(1).to_broadcast([128, H, 64]), ALU.mult, [src_b, gain.b], [src_b])

        def light(x_src, nblk, blk0, cos_src, sin_src, first, kvk_out, treal):
            Tn = 128 * nblk
            XR, XC, RR, II = C.L_xr, C.L_xc, C.L_r, C.L_i
            for n in range(nblk):
                S.dma("sp", C.xt.t[:], x_src[n * 128:(n + 1) * 128, :], writes=[C.xt.b])
                norm_transpose(C.xt.t[:], C.xt.b, C_GMIX, C.hT, n * 128)
                yield
            cp("pool", XR.t[:, :, 0:3], C.xtail.t[:, :, :], [C.xtail.b], [XR.b])
            for half in range(2):
                ps = C.ps_mm[0]
                first_mm = True
                for qq in range(2):
                    q = half * 2 + qq
                    for k in range(8):
                        mm(ps.t[:, qq * Tn:(qq + 1) * Tn], C.wl.t[:, k, q * 128:(q + 1) * 128], C.hT.t[:, k, 0:Tn],
                           first_mm, k == 7, [C.wl.b, C.hT.b], [ps.b])
                        first_mm = False
                act(XR.t[:, 2 * half:2 * half + 2, 3:3 + Tn], ps.t[:, 0:2 * Tn].rearrange("p (a b) -> p a b", a=2),
                    AF.Identity, [ps.b], [XR.b])
                yield
            for n in range(nblk):
                S.dma("sp", C.cs.t[:, 0, :], cos_src[n * 128:(n + 1) * 128, :], writes=[C.cs.b])
                S.dma("sp", C.cs.t[:, 1, :], sin_src[n * 128:(n + 1) * 128, :], writes=[C.cs.b])
                ps = C.ps_mm[0]
                for k in range(8):
                    mm(ps.t[:, 0:320], C.hT.t[:, k, n * 128:(n + 1) * 128], C.wl.t[:, k, 512:832], k == 0, k == 7,
                       [C.wl.b, C.hT.b], [ps.b])
                act(C.kv.t[:], ps.t[:, 0:320], AF.Identity, [ps.b], [C.kv.b])
                kk = C.kv.t[:, 0:128].rearrange("p (h d) -> p h d", h=2)
                head_rmsnorm("dve", kk, 2, C.gk, C.kv.b)
                rope("dve", kk, C.kout.t[:, 0:128].rearrange("p (h d) -> p h d", h=2), 2,
                     C.cs.t[:, 0, :], C.cs.t[:, 1, :], [C.kv.b, C.cs.b], [C.kout.b])
                rope("dve", C.kv.t[:, 256:320].rearrange("p (h d) -> p h d", h=1),
                     C.kout.t[:, 256:320].rearrange("p (h d) -> p h d", h=1), 1,
                     C.cs.t[:, 0, :], C.cs.t[:, 1, :], [C.kv.b, C.cs.b], [C.kout.b])
                cp("pool", C.kout.t[:, 128:256], C.kv.t[:, 128:256], [C.kv.b], [C.kout.b])
                S.dma("sp", kvk_out[n * 128:(n + 1) * 128, :], C.kout.t[:], reads=[C.kout.b], writes=[])
                blk = blk0 + n
                cp("act", C.kb16.t[:, 0:128], C.kout.t[:, 0:128], [C.kout.b], [C.kb16.b])
                cp("act", C.kb16.t[:, 128:192], C.kout.t[:, 256:320], [C.kout.b], [C.kb16.b])
                cp("act", C.kb16.t[:, 192:256], C.kout.t[:, 256:320], [C.kout.b], [C.kb16.b])
                cp("pool", C.Vc.t[:, blk, :, 0:64], C.kv.t[:, 128:256].rearrange("p (h d) -> p h d", h=2),
                   [C.kv.b], [C.Vc.bb[blk]])
                for j in range(2):
                    tr(C.ps_tr.t[:, j, :], C.kb16.t[:, j * 128:(j + 1) * 128], [C.kb16.b], [C.ps_tr.b])
                cp("act", C.kT.t[:, blk * 128:(blk + 1) * 128], C.ps_tr.t[:, 0, :], [C.ps_tr.b], [C.kT.bb[blk]])
                cp("act", C.kiT.t[:, blk * 128:(blk + 1) * 128], C.ps_tr.t[:, 1, :], [C.ps_tr.b], [C.kiT.bb[blk]])
                yield
            for q in range(4):
                ts("dve", XC.t[:, q, 0:Tn], XR.t[:, q, 0:Tn], col(C_CW + q), col(C_CB + q), ALU.mult, ALU.add,
                   [XR.b, C.colp.b], [XC.b])
                for j in range(1, 4):
                    stt(XC.t[:, q, 0:Tn], XR.t[:, q, j:j + Tn], col(C_CW + 4 * j + q), XC.t[:, q, 0:Tn], ALU.mult, ALU.add,
                        [XR.b, XC.b, C.colp.b], [XC.b])
            cp("pool", C.xtail.t[:, :, :], XR.t[:, :, treal:treal + 3], [XR.b], [C.xtail.b])
            cp("pool", C.xcb.t[:, :, 0:Tn], XC.t[:, :, 0:Tn], [XC.b], [C.xcb.b])
            yield
            for (wbd, hbcol, dst) in ((C.wrg, C_HBRG, RR), (C.wig, C_HBIG, II)):
                for half in range(2):
                    first_mm = True
                    for qq in range(2):
                        q = 2 * half + qq
                        mm(C.ps_mm[0].t[:, qq * Tn:(qq + 1) * Tn], wbd.t[:, q, :], C.xcb.t[:, q, 0:Tn], first_mm, True,
                           [wbd.b, C.xcb.b], [C.ps_mm[0].b])
                        first_mm = False
                    for qq in range(2):
                        q = 2 * half + qq
                        act(dst.t[:, q, 0:Tn], C.ps_mm[0].t[:, qq * Tn:(qq + 1) * Tn], AF.Tanh, [C.ps_mm[0].b, C.colp.b], [dst.b],
                            scale=0.5, bias=col(hbcol + q))
                yield
            for q in range(4):
                act(RR.t[:, q, 0:Tn], RR.t[:, q, 0:Tn], AF.Exp, [RR.b, C.colp.b], [RR.b], scale=col(C_HNSP + q), bias=col(C_HNSP + q))
            M = XR.t[:, :, 0:Tn]
            tt("pool", M, RR.t[:, :, 0:Tn], RR.t[:, :, 0:Tn], ALU.mult, [RR.b], [XR.b])
            act(M, M, AF.Ln, [XR.b], [XR.b], scale=-1.0, bias=1.0)
            act(M, M, AF.Exp, [XR.b], [XR.b], scale=0.5)
            if first:
                memset("pool", XR.t[:, :, 0:1], 1.0, [XR.b])
            yield
            stt(II.t[:, :, 0:Tn], II.t[:, :, 0:Tn], 1.0, XC.t[:, :, 0:Tn], ALU.add, ALU.mult, [II.b, XC.b], [II.b])
            stt(II.t[:, :, 0:Tn], II.t[:, :, 0:Tn], 0.5, M, ALU.mult, ALU.mult, [II.b, XR.b], [II.b])
            for q in range(4):
                S.op("dve", lambda e, q=q: e.tensor_tensor_scan(out=XC.t[:, q, 0:Tn], data0=RR.t[:, q, 0:Tn],
                                                               data1=II.t[:, q, 0:Tn], initial=C.hstate.t[:, q:q + 1],
                                                               op0=ALU.mult, op1=ALU.add),
                     reads=[RR.b, II.b, C.hstate.b], writes=[XC.b])
            cp("pool", C.hstate.t[:, :], XC.t[:, :, treal - 1], [XC.b], [C.hstate.b])
            yield

        def make_ya(gi, sample, yT):
            HS = C.L_xc
            if sample:
                hsel = HS.t[:, :, 0:128]
            else:
                tt("pool", HS.t[:, :, 128:256], HS.t[:, :, 128:256], HS.t[:, :, 0:128], ALU.subtract, [HS.b], [HS.b])
                stt(HS.t[:, :, 128:256], HS.t[:, :, 128:256], col(C_PAR), HS.t[:, :, 0:128], ALU.mult, ALU.add,
                    [HS.b, C.colp.b], [HS.b])
                hsel = HS.t[:, :, 128:256]
            stt(yT.t[:, 0:4, gi * 128:(gi + 1) * 128], hsel, 0.5, C.sgT.t[:, :, gi * 128:(gi + 1) * 128],
                ALU.mult, ALU.mult, [HS.b, C.sgT.b], [yT.b])

        def stream(src_ap, nk=8):
            w = next_wb()
            S.dma("pool", w.t[:, 0:nk, :], src_ap, writes=[w.b])
            return w

        def own_prefetch(x_src, ng):
            S.dma("sp", C.xg.t[:, 0:ng, :], x_src.rearrange("(n p) d -> p n d", p=128), writes=[C.xg.b])
            return [stream(I.w_own[2]), stream(I.w_own[0])]

        def own_proj(x_src, ng, cos_src, sin_src, qT, pre=None, parts=("norm", "wi", "qi", "gate", "q"), bank0=False):
            xg = C.xg
            Wn = ng * 128
            qf = C.ft[0]
            if "norm" in parts:
                if pre is None:
                    S.dma("sp", xg.t[:, 0:ng, :], x_src.rearrange("(n p) d -> p n d", p=128), writes=[xg.b])
                for gi in range(ng):
                    norm_transpose(xg.t[:, gi, :], xg.b, C_GMIX, C.hTg, gi * 128)
            if "wi" in parts:
                for gi in range(ng):
                    for k in range(8):
                        mm(C.ps_g.t[:, 0:8], C.hTg.t[:, k, gi * 128:(gi + 1) * 128], C.wwi.t[:, k, :], k == 0, k == 7,
                           [C.wwi.b, C.hTg.b], [C.ps_g.b])
                    act(C.w8g.t[:, gi, :], C.ps_g.t[:, 0:8], AF.Identity, [C.ps_g.b], [C.w8g.b],
                        scale=float((8 ** -0.5) * (64 ** -0.5)))

            def qpiece(piece, w):
                for gi in range(ng):
                    ps = C.ps_mm[0] if bank0 else C.ps_mm[gi % 2]
                    for k in range(8):
                        mm(ps.t[:, 0:512], C.hTg.t[:, k, gi * 128:(gi + 1) * 128], w.t[:, k, :], k == 0, k == 7,
                           [w.b, C.hTg.b], [ps.b])
                    act(qf.t[:], ps.t[:, 0:512], AF.Identity, [ps.b], [qf.b])
                    q3 = qf.t[:].rearrange("p (h d) -> p h d", h=8)
                    S.dma("sp", C.cso.t[:, 0, :], cos_src[gi * 128:(gi + 1) * 128, :], writes=[C.cso.b])
                    S.dma("sp", C.cso.t[:, 1, :], sin_src[gi * 128:(gi + 1) * 128, :], writes=[C.cso.b])
                    if piece == 1:
                        head_rmsnorm("dve", q3, 8, C.gq, qf.b)
                    rope("dve", q3, q3, 8, C.cso.t[:, 0, :], C.cso.t[:, 1, :], [qf.b, C.cso.b], [qf.b])
                    cp("act", C.t16.t[:], qf.t[:], [qf.b], [C.t16.b])
                    for j in range(4):
                        tr(C.ps_tr.t[:, j, :], C.t16.t[:, j * 128:(j + 1) * 128], [C.t16.b], [C.ps_tr.b])
                    if piece == 1:
                        cp("act", qT.t[0:64, 0:4, gi * 128:(gi + 1) * 128], C.ps_tr.t[0:64, 0:4, :], [C.ps_tr.b], [qT.b])
                        cp("act", qT.t[64:128, 4:8, gi * 128:(gi + 1) * 128], C.ps_tr.t[64:128, 0:4, :], [C.ps_tr.b], [qT.b])
                    else:
                        cp("act", C.qiTg.t[:, :, gi * 128:(gi + 1) * 128], C.ps_tr.t[:, 0:4, :], [C.ps_tr.b], [C.qiTg.b])
            if "qi" in parts:
                qpiece(2, pre[0] if pre is not None else stream(I.w_own[2]))
            if "gate" in parts:
                w = pre[1] if pre is not None else stream(I.w_own[0])
                for half in range(2):
                    ps = C.ps_mm[0] if bank0 else C.ps_mm[half]
                    first_mm = True
                    for qq in range(2):
                        q = half * 2 + qq
                        for k in range(8):
                            mm(ps.t[:, qq * Wn:(qq + 1) * Wn], w.t[:, k, q * 128:(q + 1) * 128], C.hTg.t[:, k, 0:Wn],
                               first_mm, k == 7, [w.b, C.hTg.b], [ps.b])
                            first_mm = False
                    f0, f1 = C.ft[0], C.ft[1]
                    pv = ps.t[:, 0:2 * Wn]
                    act(f0.t[:, 0:2 * Wn], pv, AF.Square, [ps.b], [f0.b])
                    ts("dve", f0.t[:, 0:2 * Wn], f0.t[:, 0:2 * Wn], 0.044715, 1.0, ALU.mult, ALU.add, [f0.b], [f0.b])
                    tt("dve", f0.t[:, 0:2 * Wn], f0.t[:, 0:2 * Wn], pv, ALU.mult, [f0.b, ps.b], [f0.b])
                    act(f1.t[:, 0:2 * Wn], f0.t[:, 0:2 * Wn], AF.Tanh, [f0.b], [f1.b], scale=0.7978845608028654)
                    stt(C.sgT.t[:, 2 * half:2 * half + 2, 0:Wn], f1.t[:, 0:2 * Wn].rearrange("p (a b) -> p a b", a=2), 1.0,
                        pv.rearrange("p (a b) -> p a b", a=2), ALU.add, ALU.mult, [f1.b, ps.b], [C.sgT.b])
            if "q" in parts:
                qpiece(1, stream(I.w_own[1]))

        def indexer(gi, nkb, kb0=0):
            qs = slice(gi * 128, (gi + 1) * 128)
            ngrp = (nkb + 3) // 4
            sc = C.arena.t
            for g in range(ngrp):
                nb_ = min(4, nkb - 4 * g)
                N = nb_ * 128
                c0 = g * 512
                kib = [C.kiT.bb[kb0 + 4 * g + i] for i in range(nb_)]
                k0 = kb0 * 128 + c0
                scb = C.arena.bb[g]
                for h in range(8):
                    j, hf = h // 2, h % 2
                    pr = slice(64 * hf, 64 * hf + 64)
                    ps = (C.ps_mm[1], C.ps_g, C.ps_pv[0], C.ps_pv[1])[h % 4]
                    mm(ps.t[:, 0:N], C.qiTg.t[pr, j, qs], C.kiT.t[pr, k0:k0 + N], True, True, [C.qiTg.b] + kib, [ps.b])
                    rl = C.rl[h % 2]
                    act(rl.t[:, 0:N], ps.t[:, 0:N], AF.Relu, [ps.b], [rl.b])
                    if h == 0:
                        ts("dve", sc[:, c0:c0 + N], rl.t[:, 0:N], C.w8g.t[:, gi, 0:1], None, ALU.mult, None,
                           [rl.b, C.w8g.b], [scb])
                    else:
                        stt(sc[:, c0:c0 + N], rl.t[:, 0:N], C.w8g.t[:, gi, h:h + 1], sc[:, c0:c0 + N], ALU.mult, ALU.add,
                            [rl.b, C.w8g.b, scb], [scb])
                    yield

        def bisect(nkb, cmask_b):
            S_all = nkb * 128
            ngrp = (nkb + 3) // 4
            sc = C.arena.t
            SB = C.arena.bb[0:ngrp]
            B = C.bis.t
            bb = C.bis.b
            A = C.bisA.t
            ab = C.bisA.b
            nD = max(128, int(round(BIS_DVE * nkb)) * 128)
            nA = S_all - nD
            S.op("dve", lambda e: e.tensor_reduce(out=B[:, 0:1], in_=sc[:, 0:S_all], axis=AX.X, op=ALU.max),
                 reads=SB, writes=[bb])
            S.op("dve", lambda e: e.tensor_reduce(out=B[:, 1:2], in_=sc[:, 0:S_all], axis=AX.X, op=ALU.min),
                 reads=SB, writes=[bb])
            lastg = [C.arena.bb[g] for g in sorted(set([(nkb - 2) // 4, (nkb - 1) // 4]))]
            tt("dve", sc[:, S_all - 256:S_all], sc[:, S_all - 256:S_all], C.cm.t[:], ALU.add, lastg + [C.cm.b], lastg)
            yield
            tt("dve", B[:, 2:3], B[:, 0:1], B[:, 1:2], ALU.subtract, [bb], [bb])
            ts("dve", B[:, 3:4], B[:, 2:3], 1.0 + 2.0 / 64, 2e-12, ALU.mult, ALU.add, [bb], [bb])
            ts("dve", B[:, 4:5], B[:, 2:3], -1.0 / 64, -1e-12, ALU.mult, ALU.add, [bb], [bb])
            tt("dve", B[:, 4:5], B[:, 4:5], B[:, 1:2], ALU.add, [bb], [bb])
            ts("dve", B[:, 12:12 + NIT], C.pw.t[:], B[:, 3:4], None, ALU.mult, None, [bb, C.pw.b], [bb])
            thr = TOPK - 0.5 - 0.5 * nA
            MID, CNT, GG = C.bmid, C.bcnt, C.bg
            tt("dve", MID.t[:, 0:1], B[:, 12:13], B[:, 4:5], ALU.add, [bb], [MID.b])
            for it in range(NIT):
                last = it == NIT - 1
                if nA > 0:
                    act(C.jk.t[:, 2:3].to_broadcast([128, nA]), sc[:, nD:S_all], AF.Sign, SB + [MID.b], [ab],
                        scale=-1.0, bias=MID.t[:, 0:1], accum_out=A[:, 0:1])
                    act(A[:, 1:2], A[:, 0:1], AF.Identity, [ab], [ab], scale=0.5, bias=float(thr))
                else:
                    memset("pool", A[:, 1:2], float(thr), [ab])
                S.op("dve", lambda e: e.tensor_scalar(out=C.jk.t[:, 4:5].to_broadcast([128, nD]), in0=sc[:, 0:nD], scalar1=MID.t[:, 0:1],
                                                      scalar2=0.0, op0=ALU.is_ge, op1=ALU.add, accum_out=CNT.t[:, 0:1]),
                     reads=SB + [MID.b], writes=[CNT.b])
                sn = B[:, 12 + it + 1:13 + it + 1] if not last else B[:, 12 + it:13 + it]
                tt("pool", C.bisP.t[:, 0:1], MID.t[:, 0:1], sn, ALU.subtract, [MID.b, bb], [C.bisP.b])
                ts("dve", GG.t[:, 0:1], CNT.t[:, 0:1], A[:, 1:2], sn, ALU.is_ge, ALU.mult, [CNT.b, bb, ab], [GG.b])
                if not last:
                    stt(MID.t[:, 0:1], GG.t[:, 0:1], 2.0, C.bisP.t[:, 0:1], ALU.mult, ALU.add, [GG.b, C.bisP.b], [MID.b])
                else:
                    tt("dve", B[:, 4:5], GG.t[:, 0:1], C.bisP.t[:, 0:1], ALU.add, [GG.b, C.bisP.b], [bb])
                yield

        def maskgen(nkb):
            sc = C.arena.t
            B = C.bis.t
            ngrp = (nkb + 3) // 4
            for g in range(ngrp):
                c0 = g * 512
                N = min(4, nkb - 4 * g) * 128
                ts("dve", C.mball.t[:, c0:c0 + N], sc[:, c0:c0 + N], B[:, 4:5], col(C_NEG), ALU.is_lt, ALU.mult,
                   [C.arena.bb[g], C.bis.b, C.colp.b], [C.mball.b], saturate=False)

        def attention(gi, nkb, yT, qT, kb0=0):
            qs = slice(gi * 128, (gi + 1) * 128)
            sc = C.arena.t
            B = C.bis.t
            bb = C.bis.b

            def sbank(kb, half):
                return C.ps_s[half] if kb % 2 == 0 else (C.ps_mm[1], C.ps_g)[half]

            def qkm(kb, half):
                ks = slice(kb * 128, (kb + 1) * 128)
                ps = sbank(kb, half)
                o = ps.t[:, :].rearrange("p (a b) -> p a b", a=4)
                kc = slice((kb0 + kb) * 128, (kb0 + kb + 1) * 128)
                mm(o, C.kT.t[:, kc], qT.t[:, 4 * half:4 * half + 4, qs], True, False, [C.kT.bb[kb0 + kb], qT.b], [ps.b])
                mm(o, C.mball.t[:, ks], C.ident4.t[:, :, :], False, True, [C.mball.b, C.ident4.b], [ps.b])

            def expo(kb, half):
                pT = C.pT[2 * (kb % 2) + half]
                act(pT.t[:], sbank(kb, half).t[:], AF.Exp, [sbank(kb, half).b], [pT.b], scale=0.125)

            def pvm(kb, half):
                pT = C.pT[2 * (kb % 2) + half]
                pv = C.ps_pv[half]
                mm(pv.t[0:65, 0:512], C.Vc.t[:, kb0 + kb, half, :], pT.t[:, :], kb == 0, kb == nkb - 1,
                   [pT.b, C.Vc.bb[kb0 + kb]], [pv.b])
            oodd = C.t16.t[:, :].rearrange("p (a b) -> p a b", a=4)
            for half in range(2):
                qkm(0, half)
                expo(0, half)
            for kb in range(nkb):
                for half in range(2):
                    if kb + 1 < nkb:
                        qkm(kb + 1, half)
                    pvm(kb, half)
                    if kb + 1 < nkb:
                        expo(kb + 1, half)
                    yield
            for half in range(2):
                pv = C.ps_pv[half]
                rd = C.rl[half]
                ps = C.ps_s[half]
                S.op("dve", lambda e, pv=pv, rd=rd: e.reciprocal(out=rd.t[64:65, 0:512], in_=pv.t[64:65, 0:512]),
                     reads=[pv.b], writes=[rd.b])
                mm(ps.t[0:64, 0:512], C.onesf.t[64:65, 0:64], rd.t[64:65, 0:512], True, True, [C.onesf.b, rd.b], [ps.b])
                cp("act", rd.t[0:64, 0:512], ps.t[0:64, 0:512], [ps.b], [rd.b])
                pv4 = pv.t[0:64, 0:512].rearrange("p (a b) -> p a b", a=4)
                rd4 = rd.t[0:64, 0:512].rearrange("p (a b) -> p a b", a=4)
                tt("dve", yT.t[0:64, 4 + 2 * half:6 + 2 * half, qs], pv4[:, 0:4:2, :], rd4[:, 0:4:2, :], ALU.mult,
                   [pv.b, rd.b], [yT.b])
                tt("dve", oodd[0:64, 2 * half:2 * half + 2, :], pv4[:, 1:4:2, :], rd4[:, 1:4:2, :], ALU.mult,
                   [pv.b, rd.b], [C.t16.b])
            S.dma("sp", yT.t[64:128, 4:8, qs], oodd[0:64, :, :], reads=[C.t16.b], writes=[yT.b])
            yield

        def dense(ng, x_src, y_dst, yT):
            Wn = ng * 128
            xg = C.xg
            BK = (C.ps_mm[0], C.ps_s[0])
            PO = C.ps_s[1]
            S.dma("sp", xg.t[:, 0:ng, :], x_src.rearrange("(n p) d -> p n d", p=128), writes=[xg.b])
            cnt = 0
            for ch in range(2):
                w = stream(I.w_o[ch])
                for gi in range(ng):
                    PS = BK[cnt % 2]
                    cnt += 1
                    for k in range(8):
                        mm(PS.t[:, 0:512], yT.t[:, k, gi * 128:(gi + 1) * 128], w.t[:, k, :], k == 0, k == 7,
                           [w.b, yT.b], [PS.b])
                    xs = xg.t[:, gi, ch * 512:(ch + 1) * 512]
                    tt("dve", xs, PS.t[:, 0:512], xs, ALU.add, [PS.b, xg.b], [xg.b])
                    yield
            for gi in range(ng):
                norm_transpose(xg.t[:, gi, :], xg.b, C_GFFN, C.hTg, gi * 128)
                yield
            NU = 2

            def epilogue(PS, fl):
                fa, fb = C.ft[0], C.ft[1]
                act(fa.t[:, 0:Wn], PS.t[:, 0:Wn], AF.Tanh, [PS.b], [fa.b], scale=0.5)
                stt(fb.t[:, 0:Wn], fa.t[:, 0:Wn], 1.0, PS.t[:, 0:Wn], ALU.add, ALU.mult, [fa.b, PS.b], [fb.b])
                stt(C.actT.t[:, fl, 0:Wn], fb.t[:, 0:Wn], 0.5, PS.t[:, Wn:2 * Wn], ALU.mult, ALU.mult,
                    [fb.b, PS.b], [C.actT.b])
            for u0 in range(0, 11, NU):
                u1 = min(11, u0 + NU)
                f0 = 2 * u0
                nf = 2 * (u1 - u0)
                pend = None
                for u in range(u0, u1):
                    w = stream(I.w_fi[u])
                    for e_ in range(2):
                        fl = 2 * (u - u0) + e_
                        PS = BK[cnt % 2]
                        cnt += 1
                        for k in range(8):
                            mm(PS.t[:, 0:Wn], w.t[:, k, e_ * 256:e_ * 256 + 128], C.hTg.t[:, k, 0:Wn], k == 0, k == 7,
                               [w.b, C.hTg.b], [PS.b])
                        for k in range(8):
                            mm(PS.t[:, Wn:2 * Wn], w.t[:, k, e_ * 256 + 128:e_ * 256 + 256], C.hTg.t[:, k, 0:Wn], False, k == 7,
                               [w.b, C.hTg.b], [PS.b])
                        if pend is not None:
                            epilogue(*pend)
                        pend = (PS, fl)
                        yield
                epilogue(*pend)
                for ch in range(2):
                    w = stream(I.w_fo[ch][:, f0:f0 + nf, :], nk=nf)
                    for gi in range(ng):
                        for fl in range(nf):
                            mm(PO.t[:, 0:512], C.actT.t[:, fl, gi * 128:(gi + 1) * 128], w.t[:, fl, :],
                               fl == 0, fl == nf - 1, [w.b, C.actT.b], [PO.b])
                        xs = xg.t[:, gi, ch * 512:(ch + 1) * 512]
                        tt("dve", xs, PO.t[:, 0:512], xs, ALU.add, [PO.b, xg.b], [xg.b])
                        yield
            S.dma("sp", y_dst.rearrange("(n p) d -> p n d", p=128), xg.t[:, 0:ng, :], reads=[xg.b], writes=[])
            yield

        def store_state(dst):
            S.dma("sp", dst[0].rearrange("(c p) -> p c", p=128), C.hstate.t[:, :], reads=[C.hstate.b], writes=[],
                  allow_slow_non_contiguous=True)
            for r in range(3):
                S.dma("sp", dst[1 + r].rearrange("(c p) -> p c", p=128), C.xtail.t[:, :, r], reads=[C.xtail.b], writes=[],
                      allow_slow_non_contiguous=True)

        def drain(gen):
            for _ in gen:
                pass

        NSLOT = NB // 2
        items = []
        if with_sample:
            items.append(dict(kind="smp"))
        for slot in range(NSLOT):
            items.append(dict(kind="slot", slot=slot))

        def light_slot(slot):
            r0 = slot * 256
            return light(I.x_all[r0:r0 + 256, :], 2, 2 * slot, I.cos_all[r0:r0 + 256, :], I.sin_all[r0:r0 + 256, :], slot == 0,
                         O.kvk_all[r0:r0 + 256, :], 256)

        def prompt_reset():
            memset("pool", C.hstate.t[:], 0.0, [C.hstate.b])
            memset("pool", C.xtail.t[:], 0.0, [C.xtail.b])
            yield

        if with_sample:
            NPB = PAST // 128
            S.dma("sp", C.cm.t[:], I.cm_smp, writes=[C.cm.b])
            stg = C.wb[0]
            sv = stg.t[:].rearrange("p k (a c) -> p (k a) c", c=128)
            cks = I.cache_k.rearrange("(b p) c -> p b c", p=128)
            for r0 in range(0, NPB, 8):
                S.dma("pool", sv[:, r0:r0 + 8, :], cks[:, r0:r0 + 8, :], writes=[stg.b])
            for r0 in range(0, NPB, 8):
                for j in range(8):
                    tr(C.ps_tr.t[:, j, :], sv[:, r0 + j, :], [stg.b], [C.ps_tr.b])
                cp("act", C.kT.t[:, (SMP_OFF + r0) * 128:(SMP_OFF + r0 + 8) * 128], C.ps_tr.t[:].rearrange("p a b -> p (a b)"), [C.ps_tr.b],
                   C.kT.bb[SMP_OFF + r0:SMP_OFF + r0 + 8])
            stg2 = C.wb[1]
            sv2 = stg2.t[:].rearrange("p k (a c) -> p (k a) c", c=128)
            kis = I.cache_ki.rearrange("(b p) c -> p b c", p=128)
            for r0 in range(0, NPB, 8):
                S.dma("pool", sv2[:, r0:r0 + 8, 0:64], kis[:, r0:r0 + 8, :], writes=[stg2.b])
                S.dma("pool", sv2[:, r0:r0 + 8, 64:128], kis[:, r0:r0 + 8, :], writes=[stg2.b])
            for r0 in range(0, NPB, 8):
                for j in range(8):
                    tr(C.ps_tr.t[:, j, :], sv2[:, r0 + j, :], [stg2.b], [C.ps_tr.b])
                cp("act", C.kiT.t[:, (SMP_OFF + r0) * 128:(SMP_OFF + r0 + 8) * 128], C.ps_tr.t[:].rearrange("p a b -> p (a b)"), [C.ps_tr.b],
                   C.kiT.bb[SMP_OFF + r0:SMP_OFF + r0 + 8])
            cvs = I.cache_v.rearrange("(b p) (h d) -> p b h d", p=128, h=2)
            for r0 in range(0, NPB, 8):
                for hh in range(2):
                    S.dma("pool", C.Vc.t[:, SMP_OFF + r0:SMP_OFF + r0 + 8, hh, 0:64], cvs[:, r0:r0 + 8, hh, :],
                          writes=C.Vc.bb[SMP_OFF + r0:SMP_OFF + r0 + 8])
            S.dma("sp", C.hstate.t[:, :], I.state_h.rearrange("(c p) -> p c", p=128), writes=[C.hstate.b],
                  allow_slow_non_contiguous=True)
            for r in range(3):
                S.dma("sp", C.xtail.t[:, :, r], I.state_conv[r].rearrange("(c p) -> p c", p=128), writes=[C.xtail.b],
                      allow_slow_non_contiguous=True)
            drain(light(I.x_smp, 1, SMP_OFF + NPB, I.cos_smp, I.sin_smp, False, O.kvk_smp, DEC))
            store_state(O.hc_s)
        else:
            drain(prompt_reset())
            drain(light_slot(0))

        prevY = None
        cur_dense = [None, 0]
        own_pre = None
        pend_dense = None
        for idx, it in enumerate(items):
            if it["kind"] == "smp":
                par, gi, nkb, ng, kb0 = 1, 0, PAST // 128 + 1, 1, SMP_OFF
                own_proj(I.x_smp, 1, I.cos_smp, I.sin_smp, C.qTg[1])
                make_ya(0, True, C.yTg[1])
                new_group = True
            else:
                slot = it["slot"]
                grp, gi = slot // G, slot % G
                par = grp % 2
                kb0 = 0
                nkb = 2 * slot + 2
                new_group = gi == 0
                late_own = None
                if new_group:
                    xo, co, so = (I.x_own[grp * W:(grp + 1) * W, :], I.cos_own[grp * W:(grp + 1) * W, :],
                                  I.sin_own[grp * W:(grp + 1) * W, :])
                    own_proj(xo, G, co, so, C.qTg[par], pre=own_pre, parts=("norm", "wi", "qi"))

                    def late_own(xo=xo, co=co, so=so, par=par, pre=own_pre):
                        own_proj(xo, G, co, so, C.qTg[par], pre=pre, parts=("gate", "q"), bank0=True)
                        yield
                    own_pre = None
                if slot == 0:
                    S.dma("sp", C.cm.t[:], I.cm_own, writes=[C.cm.b])
                if late_own is None:
                    make_ya(gi, False, C.yTg[par])

            def step(gen):
                try:
                    next(gen)
                    return True
                except StopIteration:
                    return False
            def step(gen):
                try:
                    next(gen)
                    return True
                except StopIteration:
                    return False
            if pend_dense is not None and not new_group:
                cur_dense[0] = pend_dense()
                cur_dense[1] = 54
                pend_dense = None
            others = []
            if idx + 1 < len(items):
                if it["kind"] == "smp":
                    others.append(prompt_reset())
                others.append(light_slot(items[idx + 1]["slot"]))
            dense_budget = 0
            if cur_dense[0] is not None:
                dense_budget = 10 ** 6
            def record(gens):
                rec = []
                real_op, real_dma = S.op, S.dma
                S.op = lambda *a_, **k_: rec.append((real_op, a_, k_))
                S.dma = lambda *a_, **k_: rec.append((real_dma, a_, k_))
                try:
                    for g_ in gens:
                        for _ in g_:
                            pass
                finally:
                    S.op, S.dma = real_op, real_dma
                return rec
            micro = []
            if it["kind"] == "slot" and late_own is not None:
                micro = record([late_own()])
            dgen = cur_dense[0]
            cur_dense[0] = None
            idx_gen = indexer(gi, nkb, kb0)
            n_idx = 8 * ((nkb + 3) // 4)
            n_light_heads = n_idx if dgen is None else max(8, n_idx // 4)
            per = len(micro) / float(n_light_heads)
            perd = (54.0 / max(1, n_idx - n_light_heads)) if dgen is not None else 0.0
            acc = 0.0
            accd = 0.0
            pos = 0
            while step(idx_gen):
                if pos < len(micro):
                    acc += per
                    while acc >= 1.0 and pos < len(micro):
                        acc -= 1.0
                        f_, a_, k_ = micro[pos]
                        pos += 1
                        f_(*a_, **k_)
                elif dgen is not None:
                    accd += perd
                    while dgen is not None and accd >= 1.0:
                        accd -= 1.0
                        if not step(dgen):
                            dgen = None
            while pos < len(micro):
                f_, a_, k_ = micro[pos]
                pos += 1
                f_(*a_, **k_)
            if dgen is not None:
                drain(dgen)
            if it["kind"] == "slot" and late_own is not None:
                make_ya(gi, False, C.yTg[par])
            if idx + 1 < len(items) and items[idx + 1]["slot"] % G == 0:
                g2 = items[idx + 1]["slot"] // G
                own_pre = own_prefetch(I.x_own[g2 * W:(g2 + 1) * W, :], G)
            att_gen, n_att = (prevY if prevY is not None else (None, 0))
            micro2 = record(list(others))
            bis_gen = bisect(nkb, None)
            acc = 0.0
            accl = 0.0
            posl = 0
            while step(bis_gen):
                acc += n_att / float(NIT + 1)
                accl += len(micro2) / float(NIT)
                while att_gen is not None and acc >= 1.0:
                    acc -= 1.0
                    if not step(att_gen):
                        att_gen = None
                while accl >= 1.0 and posl < len(micro2):
                    accl -= 1.0
                    f_, a_, k_ = micro2[posl]
                    posl += 1
                    f_(*a_, **k_)
            while att_gen is not None and step(att_gen):
                pass
            while posl < len(micro2):
                f_, a_, k_ = micro2[posl]
                posl += 1
                f_(*a_, **k_)
            prevY = None
            maskgen(nkb)
            prevY = (attention(gi, nkb, C.yTg[par], C.qTg[par], kb0), 2 * nkb + 1)
            if it["kind"] == "smp":
                pend_dense = (lambda: dense(1, I.x_smp, O.y_smp, C.yTg[1]))
            elif gi == G - 1:
                pend_dense = (lambda grp=grp, par=par: dense(G, I.x_own[grp * W:(grp + 1) * W, :],
                                                             O.y_own[grp * W:(grp + 1) * W, :], C.yTg[par]))
        drain(prevY[0])
        if cur_dense[0] is not None:
            drain(cur_dense[0])
        drain(pend_dense())
        store_state(O.hc_p)
        S.finish()
        S.emit()
        print("instructions:", S.nins, {e: len(S.prog[e]) for e in S.prog})
    return nc


def _rope_tables(pos):
    half = 32
    inv = (10000.0 ** (-np.arange(half, dtype=np.float32) / half)).astype(np.float32)
    ang = pos.astype(np.float32)[:, None] * inv[None, :]
    return np.cos(ang).astype(np.float32), np.sin(ang).astype(np.float32)


def _prep_shared(w_in, w_out, w_ffn_in, w_ffn_out, conv_w, conv_b, b_rg, b_ig, lru_lambda, norm_mix, norm_ffn):
    sh = {}
    xr, gate, q, k, v, qi, ki, wi = np.split(w_in, np.cumsum([512, 512, 512, 128, 128, 512, 64])[:], axis=1)
    sh["w_light"] = np.ascontiguousarray(np.concatenate([xr, k, v, ki], axis=1))
    qp = np.concatenate([np.concatenate([q[:, j * 64:(j + 1) * 64], q[:, (j + 4) * 64:(j + 5) * 64]], axis=1) for j in range(4)], axis=1)

    def pk(w):
        return w.reshape(8, 128, 512).transpose(1, 0, 2)
    sh["w_own"] = np.ascontiguousarray(np.stack([pk(gate), pk(qp), pk(qi)], axis=0))
    sh["w_wi"] = np.ascontiguousarray(wi)
    sh["w_o"] = np.ascontiguousarray(np.stack([pk(w_out[:, 0:512]), pk(w_out[:, 512:1024])], axis=0))
    g_, u_ = w_ffn_in[:, :DFF], w_ffn_in[:, DFF:]
    units = []
    for u in range(11):
        cols = np.concatenate([g_[:, (2 * u) * 128:(2 * u + 1) * 128], u_[:, (2 * u) * 128:(2 * u + 1) * 128],
                               g_[:, (2 * u + 1) * 128:(2 * u + 2) * 128], u_[:, (2 * u + 1) * 128:(2 * u + 2) * 128]], axis=1)
        units.append(pk(cols))
    sh["w_fi"] = np.ascontiguousarray(np.stack(units, axis=0))
    sh["w_fo"] = np.ascontiguousarray(np.stack([w_ffn_out[:, ch * 512:(ch + 1) * 512].reshape(22, 128, 512).transpose(1, 0, 2)
                                                for ch in range(2)], axis=0))
    colsrc = np.zeros((128, NCOLP), np.float32)

    def colset(c0, vec, n):
        colsrc[:, c0:c0 + n] = vec.reshape(n, 128).T
    colset(C_GMIX, norm_mix, 8)
    colset(C_GFFN, norm_ffn, 8)
    for j in range(4):
        colset(C_CW + 4 * j, conv_w[j], 4)
    colset(C_CB, conv_b, 4)
    colset(C_HBRG, b_rg.reshape(-1), 4)
    colset(C_HBIG, b_ig.reshape(-1), 4)
    colset(C_LAM, lru_lambda, 4)
    colsrc[:, C_NEG] = NEG
    sh["colsrc_base"] = colsrc
    return sh


def _run(inputs, NB, with_sample=True):
    f = lambda a: np.ascontiguousarray(np.asarray(a, dtype=np.float32))
    x_prompt = f(inputs["x_prompt"])[:, :NB * 128]
    x_sample = f(inputs["x_sample"])
    sh = _prep_shared(f(inputs["w_in"]), f(inputs["w_out"]), f(inputs["w_ffn_in"]), f(inputs["w_ffn_out"]),
                      f(inputs["conv_w"]), f(inputs["conv_b"]), f(inputs["b_rg"]), f(inputs["b_ig"]),
                      f(inputs["lru_lambda"]), f(inputs["norm_mix"]), f(inputs["norm_ffn"]))
    cos_all, sin_all = _rope_tables(np.arange(NB * 128))
    cs_, ss_ = _rope_tables(PAST + np.arange(DEC))
    cos_smp = np.zeros((128, 32), np.float32); cos_smp[:DEC] = cs_
    sin_smp = np.zeros((128, 32), np.float32); sin_smp[:DEC] = ss_
    pw = np.tile((2.0 ** -(np.arange(NIT) + 1.0)).astype(np.float32)[None, :], (128, 1))
    BIG = np.float32(-1e30)
    t = np.arange(128)[:, None]
    s = np.arange(256)[None, :]
    cm = []
    for par in range(2):
        qpos = par * 128 + t
        cm.append(np.where((qpos // 64) >= (s // 64), np.float32(0), BIG).astype(np.float32))
    cm_smp = np.where(s < 128 + DEC, np.float32(0), BIG).astype(np.float32) * np.ones((128, 1), np.float32)
    NOWN = NB // 2
    in_maps = []
    for c in range(8):
        b, par = c // 2, c % 2
        xa = x_prompt[b]
        own_idx = (np.arange(NOWN)[:, None] * 256 + par * 128 + np.arange(128)[None, :]).reshape(-1)
        colsrc = sh["colsrc_base"].copy()
        colsrc[:, C_PAR] = float(par)
        xs = np.zeros((128, D), np.float32); xs[:DEC] = x_sample[c]
        m = {
            "x_all": xa, "x_own": np.ascontiguousarray(xa[own_idx]), "x_smp": xs,
            "cache_k": f(inputs["cache_k"])[c].reshape(PAST, 128), "cache_v": f(inputs["cache_v"])[c].reshape(PAST, 128),
            "cache_ki": f(inputs["cache_kidx"])[c], "state_h": f(inputs["state_h"])[c], "state_conv": f(inputs["state_conv"])[c],
            "w_light": sh["w_light"], "w_own": sh["w_own"], "w_wi": sh["w_wi"], "w_o": sh["w_o"], "w_fi": sh["w_fi"], "w_fo": sh["w_fo"],
            "w_rg": f(inputs["w_rg"]), "w_ig": f(inputs["w_ig"]), "colsrc": colsrc,
            "gq": f(inputs["q_norm"]), "gk": f(inputs["k_norm"]), "ident": np.eye(128, dtype=np.float32), "pw": pw,
            "cos_all": cos_all, "sin_all": sin_all, "cos_own": np.ascontiguousarray(cos_all[own_idx]),
            "sin_own": np.ascontiguousarray(sin_all[own_idx]), "cos_smp": cos_smp, "sin_smp": sin_smp,
            "cm_own": cm[par], "cm_smp": cm_smp,
        }
        in_maps.append(m)
    nc = build(NB, with_sample)
    res = run_bass_kernel_spmd(nc, in_maps, core_ids=list(range(8)))
    R = res.results
    Tn = NB * 128
    y_p = np.zeros((4, Tn, D), np.float32)
    k_p = np.zeros((4, Tn, 2, 64), np.float32); v_p = np.zeros((4, Tn, 2, 64), np.float32); ki_p = np.zeros((4, Tn, 64), np.float32)
    h_p = np.zeros((4, LW), np.float32); conv_p = np.zeros((4, 3, LW), np.float32)
    y_s = np.zeros((8, DEC, D), np.float32)
    k_s = np.zeros((8, DEC, 2, 64), np.float32); v_s = np.zeros((8, DEC, 2, 64), np.float32); ki_s = np.zeros((8, DEC, 64), np.float32)
    h_s = np.zeros((8, LW), np.float32); conv_s = np.zeros((8, 3, LW), np.float32)
    for c in range(8):
        b, par = c // 2, c % 2
        r = R[c]
        y_p[b].reshape(NOWN, 2, 128, D)[:, par] = r["y_own"].reshape(NOWN, 128, D)
        if par == 0:
            kvk = r["kvk_all"]
            k_p[b] = kvk[:, 0:128].reshape(Tn, 2, 64); v_p[b] = kvk[:, 128:256].reshape(Tn, 2, 64); ki_p[b] = kvk[:, 256:320]
            h_p[b] = r["hc_p"][0]; conv_p[b] = r["hc_p"][1:4]
        if with_sample:
            y_s[c] = r["y_smp"][:DEC]
            kvk = r["kvk_smp"][:DEC]
            k_s[c] = kvk[:, 0:128].reshape(DEC, 2, 64); v_s[c] = kvk[:, 128:256].reshape(DEC, 2, 64); ki_s[c] = kvk[:, 256:320]
            h_s[c] = r["hc_s"][0]; conv_s[c] = r["hc_s"][1:4]
    return (y_p, y_s, k_p, v_p, ki_p, h_p, conv_p, k_s, v_s, ki_s, h_s, conv_s)


def kernel(**inputs):
    return _run(inputs, SEQ // 128, True)
```
